# Optimizing a Trainium2 kernel written in Bass

```python
import math
import jax, jax.numpy as jnp
from jax import lax
import numpy as np

D_MODEL = 1024
BATCH = 8
SEQ = 2048
DEPTH = 4

ATTN_HEADS = 6
ATTN_HEAD_DIM = 64
ATTN_WIDTH = ATTN_HEADS * ATTN_HEAD_DIM
MOBA_BLOCK = 256
MOBA_TOPK = 3
MOBA_QUERY_CHUNK = 64
SSD_HEADS = 6
SSD_HEAD_DIM = 64
SSD_WIDTH = SSD_HEADS * SSD_HEAD_DIM
SSD_GROUPS = 2
SSD_STATE = 128
SSD_CONV = 4
SSD_CHUNK = 128
XBC_WIDTH = SSD_WIDTH + 2 * SSD_GROUPS * SSD_STATE
POOL_WINDOWS = (2, 4, 8, 16)
POOL_GROUPS = len(POOL_WINDOWS)
POOL_GROUP_DIM = 64
POOL_WIDTH = POOL_GROUPS * POOL_GROUP_DIM
MIX_WIDTH = ATTN_WIDTH + SSD_WIDTH + POOL_WIDTH
IN_PROJ_WIDTH = 3 * ATTN_WIDTH + SSD_WIDTH + XBC_WIDTH + SSD_HEADS + POOL_WIDTH
FFN_HIDDEN = ((8 * D_MODEL // 3 + 255) // 256) * 256
NORM_EPS = 1e-6
NEG_INF = -1e30

kernel_name = 'hybrid_moba_ssd_pool_trunk'


def rmsnorm(x, g):
    xf = x.astype(jnp.float32)
    xf = xf * lax.rsqrt(jnp.mean(xf * xf, axis=-1, keepdims=True) + NORM_EPS)
    return xf.astype(x.dtype) * g


def causal_depthwise_conv(x, w, b):
    k_width, ch = w.shape
    y = lax.conv_general_dilated(
        x, w[:, None, :], window_strides=(1,), padding=[(k_width - 1, 0)],
        dimension_numbers=('NWC', 'WIO', 'NWC'), feature_group_count=ch)
    return y + b


def moba_attention(q, k, v):
    bsz, n_heads, seq, hd = q.shape
    n_blocks = -(-seq // MOBA_BLOCK)
    pad = n_blocks * MOBA_BLOCK - seq
    k_p = jnp.pad(k, ((0, 0), (0, 0), (0, pad), (0, 0)))
    v_p = jnp.pad(v, ((0, 0), (0, 0), (0, pad), (0, 0)))
    k_blocks = k_p.reshape(bsz, n_heads, n_blocks, MOBA_BLOCK, hd)
    v_blocks = v_p.reshape(bsz, n_heads, n_blocks, MOBA_BLOCK, hd)
    k_mean = jnp.mean(k_blocks, axis=3)
    n_sel = min(MOBA_TOPK, n_blocks)
    scale = hd ** -0.5
    b_idx = jnp.arange(bsz)[:, None, None, None]
    h_idx = jnp.arange(n_heads)[None, :, None, None]
    block_ids = jnp.arange(n_blocks)

    def query_chunk(c):
        q0 = c * MOBA_QUERY_CHUNK
        qc = lax.dynamic_slice_in_dim(q, q0, MOBA_QUERY_CHUNK, axis=2)
        blk = q0 // MOBA_BLOCK
        gate = jnp.einsum('bhqd,bhnd->bhqn', qc, k_mean).astype(jnp.float32)
        gate = jnp.where(block_ids < blk, gate, NEG_INF)
        _, idx = lax.top_k(gate, n_sel)
        valid = jnp.arange(n_sel) < blk
        k_sel = k_blocks[b_idx, h_idx, idx]
        v_sel = v_blocks[b_idx, h_idx, idx]
        s_sel = jnp.einsum('bhqd,bhqnkd->bhqnk', qc, k_sel).astype(jnp.float32) * scale
        s_sel = jnp.where(valid[:, None], s_sel, NEG_INF)
        s_sel = s_sel.reshape(bsz, n_heads, MOBA_QUERY_CHUNK, n_sel * MOBA_BLOCK)
        k_own = lax.dynamic_index_in_dim(k_blocks, blk, axis=2, keepdims=False)
        v_own = lax.dynamic_index_in_dim(v_blocks, blk, axis=2, keepdims=False)
        s_own = jnp.einsum('bhqd,bhkd->bhqk', qc, k_own).astype(jnp.float32) * scale
        q_pos = q0 + jnp.arange(MOBA_QUERY_CHUNK)
        k_pos = blk * MOBA_BLOCK + jnp.arange(MOBA_BLOCK)
        s_own = jnp.where(k_pos[None, :] <= q_pos[:, None], s_own, NEG_INF)
        p = jax.nn.softmax(jnp.concatenate([s_sel, s_own], axis=-1), axis=-1).astype(v.dtype)
        p_sel = p[..., :n_sel * MOBA_BLOCK].reshape(bsz, n_heads, MOBA_QUERY_CHUNK, n_sel, MOBA_BLOCK)
        p_own = p[..., n_sel * MOBA_BLOCK:]
        return (jnp.einsum('bhqnk,bhqnkd->bhqd', p_sel, v_sel)
                + jnp.einsum('bhqk,bhkd->bhqd', p_own, v_own))

    out = lax.map(query_chunk, jnp.arange(seq // MOBA_QUERY_CHUNK))
    return out.transpose(1, 0, 3, 2, 4).reshape(bsz, seq, n_heads * hd)


def ssd_mixer(xbc, z, dt_raw, conv_w, conv_b, dt_bias, a_log, d_skip, norm_w):
    bsz, seq, _ = xbc.shape
    n_chunks = seq // SSD_CHUNK
    rep = SSD_HEADS // SSD_GROUPS
    xbc = jax.nn.silu(causal_depthwise_conv(xbc, conv_w, conv_b)).astype(jnp.float32)
    xs, b_in, c_in = jnp.split(xbc, [SSD_WIDTH, SSD_WIDTH + SSD_GROUPS * SSD_STATE], axis=-1)
    x_h = xs.reshape(bsz, seq, SSD_HEADS, SSD_HEAD_DIM)
    b_h = jnp.repeat(b_in.reshape(bsz, seq, SSD_GROUPS, SSD_STATE), rep, axis=2)
    c_h = jnp.repeat(c_in.reshape(bsz, seq, SSD_GROUPS, SSD_STATE), rep, axis=2)
    dt = jax.nn.softplus(dt_raw.astype(jnp.float32) + dt_bias.astype(jnp.float32))
    a = -jnp.exp(a_log.astype(jnp.float32))

    def chunks(t):
        return t.reshape((bsz, n_chunks, SSD_CHUNK) + t.shape[2:])

    xdt = chunks(x_h * dt[..., None])
    b_c, c_c = chunks(b_h), chunks(c_h)
    a_cum = jnp.cumsum(chunks(dt * a).transpose(0, 3, 1, 2), axis=-1)
    causal = jnp.tril(jnp.ones((SSD_CHUNK, SSD_CHUNK), dtype=bool))
    decay = jnp.exp(jnp.where(causal, a_cum[..., :, None] - a_cum[..., None, :], -jnp.inf))
    y_diag = jnp.einsum('bclhn,bcshn,bhcls,bcshp->bclhp', c_c, b_c, decay, xdt)
    decay_to_end = jnp.exp(a_cum[..., -1:] - a_cum)
    states = jnp.einsum('bclhn,bhcl,bclhp->bchpn', b_c, decay_to_end, xdt)
    chunk_decay = jnp.exp(a_cum[..., -1])

    def step(h, inp):
        st, d = inp
        return h * d[..., None, None] + st, h

    _, prev = lax.scan(step, jnp.zeros_like(states[:, 0]),
                       (states.transpose(1, 0, 2, 3, 4), chunk_decay.transpose(2, 0, 1)))
    prev = prev.transpose(1, 0, 2, 3, 4)
    y_off = jnp.einsum('bclhn,bchpn,bhcl->bclhp', c_c, prev, jnp.exp(a_cum))
    y = (y_diag + y_off).reshape(bsz, seq, SSD_HEADS, SSD_HEAD_DIM) + x_h * d_skip.astype(jnp.float32)[:, None]
    y = y.reshape(bsz, seq, SSD_WIDTH) * jax.nn.silu(z.astype(jnp.float32))
    yg = y.reshape(bsz, seq, SSD_GROUPS, SSD_WIDTH // SSD_GROUPS)
    yg = yg * lax.rsqrt(jnp.mean(yg * yg, axis=-1, keepdims=True) + NORM_EPS)
    return (yg.reshape(bsz, seq, SSD_WIDTH) * norm_w.astype(jnp.float32)).astype(z.dtype)


def pool_mixer(p, pool_w, pool_scale):
    bsz, seq, _ = p.shape
    pg = p.astype(jnp.float32).reshape(bsz, seq, POOL_GROUPS, POOL_GROUP_DIM)
    csum = jnp.concatenate([jnp.zeros_like(pg[:, :1]), jnp.cumsum(pg, axis=1)], axis=1)
    t = jnp.arange(seq)[:, None]
    win = jnp.asarray(POOL_WINDOWS, dtype=jnp.int32)[None, :]
    start = jnp.maximum(t + 1 - win, 0)
    lower = csum[:, start, jnp.arange(POOL_GROUPS)[None, :], :]
    count = jnp.minimum(t + 1, win).astype(jnp.float32)
    mean = (csum[:, 1:] - lower) / count[None, :, :, None]
    mixed = jnp.einsum('bsgc,gcd->bsgd', mean - pg, pool_w.astype(jnp.float32))
    return (mixed.reshape(bsz, seq, POOL_WIDTH) * pool_scale.astype(jnp.float32)).astype(p.dtype)


def hybrid_layer(x, norm_mix, w_in, conv_w, conv_b, dt_bias, a_log, d_skip, ssd_norm,
                 pool_w, pool_scale, w_out, norm_ffn, w_gate_up, w_down):
    bsz, seq, _ = x.shape
    h = rmsnorm(x, norm_mix)
    u = h @ w_in
    sizes = (ATTN_WIDTH, ATTN_WIDTH, ATTN_WIDTH, SSD_WIDTH, XBC_WIDTH, SSD_HEADS, POOL_WIDTH)
    offsets = []
    acc = 0
    for s in sizes[:-1]:
        acc += s
        offsets.append(acc)
    q, k, v, z, xbc, dt_raw, p_in = jnp.split(u, offsets, axis=-1)

    def heads(t):
        return t.reshape(bsz, seq, ATTN_HEADS, ATTN_HEAD_DIM).transpose(0, 2, 1, 3)

    y_attn = moba_attention(heads(q), heads(k), heads(v))
    y_ssd = ssd_mixer(xbc, z, dt_raw, conv_w, conv_b, dt_bias, a_log, d_skip, ssd_norm)
    y_pool = pool_mixer(p_in, pool_w, pool_scale)
    x = x + jnp.concatenate([y_attn, y_ssd, y_pool], axis=-1) @ w_out
    h = rmsnorm(x, norm_ffn)
    gate, up = jnp.split(h @ w_gate_up, 2, axis=-1)
    return x + (jax.nn.silu(gate) * up) @ w_down


def setup_inputs(seed: int = 0) -> dict:
    key = jax.random.key(seed)
    ks = jax.random.split(key, 16)
    nrm = jax.random.normal
    dt0 = jnp.exp(jax.random.uniform(ks[5], (DEPTH, SSD_HEADS), minval=math.log(1e-3), maxval=math.log(1e-1)))
    return {
        'x': nrm(ks[0], (BATCH, SEQ, D_MODEL), jnp.float32),
        'norm_mix': 1.0 + 0.05 * nrm(ks[1], (DEPTH, D_MODEL), jnp.float32),
        'w_in': nrm(ks[2], (DEPTH, D_MODEL, IN_PROJ_WIDTH), jnp.float32) * D_MODEL ** -0.5,
        'conv_w': nrm(ks[3], (DEPTH, SSD_CONV, XBC_WIDTH), jnp.float32) * SSD_CONV ** -0.5,
        'conv_b': 0.02 * nrm(ks[4], (DEPTH, XBC_WIDTH), jnp.float32),
        'dt_bias': dt0 + jnp.log(-jnp.expm1(-dt0)),
        'a_log': jnp.log(jax.random.uniform(ks[6], (DEPTH, SSD_HEADS), minval=1.0, maxval=16.0)),
        'd_skip': 1.0 + 0.1 * nrm(ks[7], (DEPTH, SSD_HEADS), jnp.float32),
        'ssd_norm': 1.0 + 0.05 * nrm(ks[8], (DEPTH, SSD_WIDTH), jnp.float32),
        'pool_w': nrm(ks[9], (DEPTH, POOL_GROUPS, POOL_GROUP_DIM, POOL_GROUP_DIM), jnp.float32) * POOL_GROUP_DIM ** -0.5,
        'pool_scale': 1.0 + 0.1 * nrm(ks[10], (DEPTH, POOL_WIDTH), jnp.float32),
        'w_out': nrm(ks[11], (DEPTH, MIX_WIDTH, D_MODEL), jnp.float32) * MIX_WIDTH ** -0.5,
        'norm_ffn': 1.0 + 0.05 * nrm(ks[12], (DEPTH, D_MODEL), jnp.float32),
        'w_gate_up': nrm(ks[13], (DEPTH, D_MODEL, 2 * FFN_HIDDEN), jnp.float32) * D_MODEL ** -0.5,
        'w_down': nrm(ks[14], (DEPTH, FFN_HIDDEN, D_MODEL), jnp.float32) * FFN_HIDDEN ** -0.5,
        'norm_final': 1.0 + 0.05 * nrm(ks[15], (D_MODEL,), jnp.float32),
    }


def reference(x, norm_mix, w_in, conv_w, conv_b, dt_bias, a_log, d_skip, ssd_norm,
              pool_w, pool_scale, w_out, norm_ffn, w_gate_up, w_down, norm_final):
    for l in range(DEPTH):
        x = hybrid_layer(x, norm_mix[l], w_in[l], conv_w[l], conv_b[l], dt_bias[l], a_log[l],
                         d_skip[l], ssd_norm[l], pool_w[l], pool_scale[l], w_out[l],
                         norm_ffn[l], w_gate_up[l], w_down[l])
    return rmsnorm(x, norm_final)
```

```python
import os
import numpy as np
import concourse.bass as bass
import concourse.mybir as mybir
from concourse.bass_utils import run_bass_kernel_spmd
from contextlib import ExitStack

F32 = mybir.dt.float32
BF16 = mybir.dt.bfloat16
ALU = mybir.AluOpType
AF = mybir.ActivationFunctionType
AX = mybir.AxisListType

PE, ACT, DVE, POOL, SP = "pe", "act", "dve", "pool", "sp"
CENG = (PE, ACT, DVE, POOL)

S = 2048
D = 1024
NT = 16
NG = 4
FFN = 2816
NJ = 22
EPS = 1e-6
BIG = 30000.0
FFN_PARTS = [list(range(0, 8)), list(range(8, 15)), list(range(15, 22))]


class Buf:
    __slots__ = ("w", "rs", "name")

    def __init__(self, name=""):
        self.w = None
        self.rs = {}
        self.name = name


class Prog:
    def __init__(self, nc, es):
        self.nc = nc
        self.es = es
        self.q = {e: [] for e in CENG + (SP,)}
        self.ep = -1
        self.sem = {}
        self.n = {}
        self.sig = {}
        self.new_epoch()
        self.seen = {e: {} for e in CENG + (SP,)}
        self.nd = 0

    def new_epoch(self):
        self.ep += 1
        for e in CENG:
            k = (e, self.ep)
            self.sem[k] = self.es.enter_context(self.nc.semaphore("s_%s_%d" % k))
            self.n[k] = 0
            self.sig[k] = [False]

    def dma_sem(self, name):
        self.nd += 1
        return [self.es.enter_context(self.nc.semaphore(name)), 0, self.nd]

    def _deps(self, eng, reads, writes):
        waits = {}
        seen = self.seen[eng]

        def need(ev):
            if ev is None:
                return
            if ev[0] == "eng":
                e2, val = ev[1], ev[2]
                if e2[0] == eng and eng == PE:
                    return
                key = e2
            else:
                ds, val = ev[1], ev[2]
                key = ("d", ds[2])
            if seen.get(key, 0) >= val:
                return
            if key not in waits or waits[key][2] < val:
                waits[key] = ev

        for b in reads:
            need(b.w)
        for b in writes:
            need(b.w)
            for r in b.rs.values():
                need(r)
        for key, ev in waits.items():
            seen[key] = ev[2]
            if ev[0] == "eng":
                self.sig[ev[1]][ev[2]] = True
        return list(waits.values())

    def op(self, eng, fn, reads=(), writes=()):
        waits = self._deps(eng, reads, writes)
        k = (eng, self.ep)
        self.n[k] += 1
        self.sig[k].append(os.environ.get("LAZY", "1") != "1")
        ev = ("eng", k, self.n[k])
        self.q[eng].append((waits, fn, ev))
        for b in reads:
            b.rs[eng] = ev
        for b in writes:
            b.w = ev
            b.rs = {}

    def dma(self, qeng, fn, dsem, reads=(), writes=()):
        waits = self._deps(qeng, reads, writes)
        dsem[1] += 16
        ev = ("dma", dsem, dsem[1])
        self.q[qeng].append((waits, fn, ev))
        for b in reads:
            b.rs[("d", dsem[2])] = ev
        for b in writes:
            b.w = ev
            b.rs = {}

    def fence(self):
        last = []
        for e in CENG:
            k = (e, self.ep)
            if self.n[k] > 0:
                b = Buf()
                b.w = ("eng", k, self.n[k])
                last.append(b)
        for e in CENG + (SP,):
            self.wait_all(e, last)

    def wait_all(self, eng, blist):
        waits = self._deps(eng, blist, ())
        self.q[eng].append((waits, None, None))

    def emit(self):
        cum = {}
        for e in self.n:
            c = [0] * (self.n[e] + 1)
            for i in range(1, self.n[e] + 1):
                c[i] = c[i - 1] + (1 if self.sig[e][i] else 0)
            cum[e] = c
        self.maxcount = {e: cum[e][-1] for e in cum}
        with self.nc.Block() as block:
            def mkb(e):
                def body(engobj):
                    for waits, fn, ev in self.q[e]:
                        for w in waits:
                            if w[0] == "eng":
                                engobj.wait_ge(self.sem[w[1]], cum[w[1]][w[2]])
                            else:
                                engobj.wait_ge(w[1][0], w[2])
                        if fn is not None:
                            ins = fn(engobj)
                            if ev[0] == "dma":
                                ins.then_inc(ev[1][0], 16)
                            elif self.sig[ev[1]][ev[2]]:
                                ins.then_inc(self.sem[ev[1]], 1)
                return body

            block.tensor(mkb(PE))
            block.scalar(mkb(ACT))
            block.vector(mkb(DVE))
            block.gpsimd(mkb(POOL))
            block.sync(mkb(SP))


def mk(f, *a, **k):
    return lambda e: getattr(e, f)(*a, **k)


IN_OFF = dict(q=0, k=384, v=768, z=1152, xbc=1536, dt=2432, p=2438)


def in_tile_cols():
    tiles = []
    for j in range(3):
        cols = (list(range(IN_OFF["k"] + 128 * j, IN_OFF["k"] + 128 * (j + 1)))
                + list(range(IN_OFF["q"] + 128 * j, IN_OFF["q"] + 128 * (j + 1)))
                + list(range(IN_OFF["v"] + 128 * j, IN_OFF["v"] + 128 * (j + 1))))
        tiles.append(("A%d" % j, cols))
    x0 = IN_OFF["xbc"]
    tiles.append(("X0", list(range(x0, x0 + 384))))
    tiles.append(("X1", list(range(x0 + 384, x0 + 768))))
    tiles.append(("X2", list(range(x0 + 768, x0 + 896))))
    tiles.append(("ZD", list(range(IN_OFF["z"], IN_OFF["z"] + 384)) + list(range(IN_OFF["dt"], IN_OFF["dt"] + 6))))
    tiles.append(("PP", list(range(IN_OFF["p"], IN_OFF["p"] + 256))))
    return tiles


OUT_TILES = [(0, 384), (384, 768), (768, 1024)]


def layer_tile_plan():
    plan = []
    for name, cols in in_tile_cols():
        plan.append((name, 8, 400 if name == "ZD" else len(cols)))
    for i, (a, b) in enumerate(OUT_TILES):
        plan.append(("O%d" % i, 8, b - a))
    for pi, part in enumerate(FFN_PARTS):
        for j in part:
            plan.append(("G%d" % j, 8, 256))
        for cp in range(4):
            plan.append(("D%d_%d" % (pi, cp), len(part), 256))
    return plan


def pack_weights(w_in, w_out, w_gate_up, w_down, L):
    plan = layer_tile_plan()
    chunks = []
    for l in range(L):
        intiles = dict(in_tile_cols())
        for name, K, n in plan:
            if name in intiles:
                W = np.zeros((D, n), np.float32)
                W[:, :len(intiles[name])] = w_in[l][:, intiles[name]]
                t = W.reshape(8, 128, n).transpose(1, 0, 2)
            elif name[0] == "O":
                a, b = OUT_TILES[int(name[1:])]
                t = w_out[l][:, a:b].reshape(8, 128, n).transpose(1, 0, 2)
            elif name[0] == "G":
                j = int(name[1:])
                W = np.concatenate([w_gate_up[l][:, j * 128:(j + 1) * 128],
                                    w_gate_up[l][:, FFN + j * 128:FFN + (j + 1) * 128]], axis=1)
                t = W.reshape(8, 128, 256).transpose(1, 0, 2)
            else:
                pi, cp = [int(v) for v in name[1:].split("_")]
                part = FFN_PARTS[pi]
                W = w_down[l][part[0] * 128:(part[-1] + 1) * 128, cp * 256:(cp + 1) * 256]
                t = W.reshape(len(part), 128, 256).transpose(1, 0, 2)
            chunks.append(np.ascontiguousarray(t).reshape(128, K * n))
    return np.ascontiguousarray(np.concatenate(chunks, axis=1), dtype=np.float32)


SMF_PER = 8 + 8 + 28 + 7 + 2
ROW_PER = 6 + 6 + 384 + 384


def pack_small(norm_mix, norm_ffn, conv_w, conv_b, pool_scale, norm_final, dt_bias, a_log, d_skip, ssd_norm, pool_w, L):
    smf = np.zeros((128, SMF_PER * L + 8), np.float32)
    rowp = np.zeros((128, ROW_PER * L), np.float32)
    pw = np.zeros((L, 128, 2, 128), np.float32)
    for l in range(L):
        o = SMF_PER * l
        smf[:, o:o + 8] = norm_mix[l].reshape(8, 128).T
        smf[:, o + 8:o + 16] = norm_ffn[l].reshape(8, 128).T
        for j in range(4):
            smf[:, o + 16 + 7 * j:o + 16 + 7 * (j + 1)] = conv_w[l][j].reshape(7, 128).T
        smf[:, o + 44:o + 51] = conv_b[l].reshape(7, 128).T
        smf[:, o + 51:o + 53] = pool_scale[l].reshape(2, 128).T
        r = ROW_PER * l
        rowp[:, r:r + 6] = dt_bias[l][None, :]
        rowp[:, r + 6:r + 12] = a_log[l][None, :]
        rowp[:, r + 12:r + 396] = np.repeat(d_skip[l], 64)[None, :]
        rowp[:, r + 396:r + 780] = ssd_norm[l][None, :]
        for c in range(2):
            for gg in range(2):
                pw[l, gg * 64:(gg + 1) * 64, c, gg * 64:(gg + 1) * 64] = pool_w[l][2 * c + gg]
    smf[:, SMF_PER * L:SMF_PER * L + 8] = norm_final.reshape(8, 128).T
    return smf, rowp, pw


class StopBuild(Exception):
    pass


def build_program(L=4, dbg=(), stop=None):
    nc = bass.Bass("TRN2", target_bir_lowering=False)
    plan = layer_tile_plan()
    lay_words = sum(K * n for _, K, n in plan)
    xT_d = nc.dram_tensor("xT", [D, S], F32, kind="ExternalInput").ap()
    wst_d = nc.dram_tensor("wst", [128, lay_words * L], F32, kind="ExternalInput").ap()
    smf_d = nc.dram_tensor("smf", [128, SMF_PER * L + 8], F32, kind="ExternalInput").ap()
    rowp_d = nc.dram_tensor("rowp", [128, ROW_PER * L], F32, kind="ExternalInput").ap()
    pw_d = nc.dram_tensor("pw", [L, 128, 2, 128], F32, kind="ExternalInput").ap()
    outT_d = nc.dram_tensor("outT", [D, S], F32, kind="ExternalOutput").ap()
    dbg_d = {}
    for name, shape in dbg:
        dbg_d[name] = nc.dram_tensor("dbg_" + name, list(shape), F32, kind="ExternalOutput").ap()

    with ExitStack() as es:
        P = Prog(nc, es)

        def sb(name, shape, dt):
            return es.enter_context(nc.sbuf_tensor(name, shape, dt))

        xT = sb("xT_sb", [128, 8, S], F32)
        xT_b = [[Buf() for _ in range(NG)] for _ in range(8)]
        hT = sb("hT_sb", [128, 8, S], BF16)
        hT_b = [[Buf() for _ in range(NG)] for _ in range(8)]
        NSLOT = 3
        SLOTW = 3200
        wsl = [sb("wslot%d" % i, [128, SLOTW], BF16) for i in range(NSLOT)]
        wsl_b = [Buf() for _ in range(NSLOT)]
        wsem = [P.dma_sem("wsem%d" % i) for i in range(NSLOT)]
        R2 = sb("R2", [128, 10, S], BF16)
        R2_b = [[Buf() for _ in range(NT)] for _ in range(10)]
        R1W = 9728
        R1 = sb("R1", [128, R1W], F32)
        GR = 64
        R1_b = [Buf() for _ in range(R1W // GR)]
        smf = sb("smf_sb", [128, SMF_PER * L + 8], F32)
        rowp = sb("rowp_sb", [128, ROW_PER], F32)
        cR = Buf("rowp")
        rsem = P.dma_sem("rsem")
        pwb = sb("pw_sb", [128, L, 2, 128], BF16)
        ident = sb("ident", [128, 128], BF16)
        onesb = sb("onesb", [128, 128], BF16)
        onesf = sb("onesf", [128, 128], F32)
        triLE = sb("triLE", [128, 128], F32)
        triLEb = sb("triLEb", [128, 128], BF16)
        triGT = sb("triGT", [128, 128], BF16)
        indB = sb("indB", [64, 8, 128], BF16)
        invcnt = sb("invcnt", [128, 2, 16], F32)
        invw = sb("invw", [128, 2], F32)
        zcol = sb("zcol", [128, 2], F32)
        cB = Buf("consts")
        cS = Buf("consts_sp")
        cP = Buf("consts_pool_dma")
        cA = [cB, cS, cP]
        ld = P.dma_sem("ld")
        ldp = P.dma_sem("ldp")
        st = P.dma_sem("st")
        sts = [P.dma_sem("st%d" % i) for i in range(3)]
        banks = [es.enter_context(nc.psum_tensor("bank%d" % i, [128, 512], F32)) for i in range(8)]
        bank_b = [Buf() for _ in range(8)]

        def r1(offw, nwords, dt=F32):
            ap = R1[:, offw:offw + nwords]
            if dt != F32:
                ap = ap.bitcast(dt)
            return ap, R1_b[offw // GR:(offw + nwords + GR - 1) // GR]

        wq = []
        off = 0
        for l in range(L):
            for name, K, n in plan:
                wq.append((l, name, K, n, off))
                off += K * n
        wstate = {"next": 0, "cur": {}}

        def w_issue():
            i = wstate["next"]
            if i >= len(wq):
                return
            l, name, K, n, off = wq[i]
            s = i % NSLOT
            P.dma(POOL, mk("dma_start", out=wsl[s][:, 0:K * n], in_=wst_d[:, off:off + K * n]), wsem[s], writes=[wsl_b[s]])
            wstate["cur"][(l, name)] = (s, K, n)
            wstate["next"] = i + 1

        def w_get(l, name):
            s, K, n = wstate["cur"][(l, name)]
            return wsl[s][:, 0:K * n].rearrange("p (k n) -> p k n", k=K), wsl_b[s]

        def w_done():
            w_issue()

        for c in range(8):
            P.dma(SP, mk("dma_start", out=xT[:, c, :], in_=xT_d[c * 128:(c + 1) * 128, :]), ld, writes=xT_b[c])
        P.dma(SP, mk("dma_start", out=smf[:], in_=smf_d), ld, writes=[cS])
        for l in range(L):
            P.dma(POOL, mk("dma_start", out=pwb[:, l, :, :], in_=pw_d[l]), ldp, writes=[cP])
        for _ in range(NSLOT):
            w_issue()
        for bb_ in [cS] + [b for c in range(8) for b in xT_b[c]]:
            bb_.w = ("dma", ld, ld[1])
        cP.w = ("dma", ldp, ldp[1])
        P.op(POOL, mk("memset", onesf[:], 1.0), writes=[cB])
        P.op(POOL, mk("memset", zcol[:], 0.0), writes=[cB])
        P.op(POOL, mk("memset", onesb[:], 1.0), writes=[cB])
        P.op(POOL, mk("affine_select", out=ident[:], in_=onesf[:], pattern=[[-1, 128]], compare_op=ALU.is_equal, fill=0.0, base=0, channel_multiplier=1), reads=cA, writes=[cB])
        P.op(POOL, mk("affine_select", out=triLE[:], in_=onesf[:], pattern=[[1, 128]], compare_op=ALU.is_ge, fill=0.0, base=0, channel_multiplier=-1), reads=cA, writes=[cB])
        P.op(POOL, mk("affine_select", out=triLEb[:], in_=onesf[:], pattern=[[1, 128]], compare_op=ALU.is_ge, fill=0.0, base=0, channel_multiplier=-1), reads=cA, writes=[cB])
        P.op(POOL, mk("affine_select", out=triGT[:], in_=onesf[:], pattern=[[-1, 128]], compare_op=ALU.is_gt, fill=0.0, base=0, channel_multiplier=1), reads=cA, writes=[cB])
        P.op(POOL, mk("memset", indB[:], 1.0), writes=[cB])
        for a in range(2):
            P.op(POOL, mk("affine_select", out=indB[32 * a:32 * a + 32, :, :], in_=indB[32 * a:32 * a + 32, :, :],
                             pattern=[[-1, 8], [0, 128]], compare_op=ALU.is_equal, fill=0.0, base=0, channel_multiplier=1), reads=cA, writes=[cB])
        for c in range(2):
            for hh in range(2):
                wv = float(2 ** (2 * c + hh + 1))
                P.op(POOL, mk("memset", invw[64 * hh:64 * hh + 64, c:c + 1], 1.0 / wv), writes=[cB])
                P.op(POOL, mk("iota", invcnt[64 * hh:64 * hh + 64, c, :], [[1, 16]], base=1, channel_multiplier=0, allow_small_or_imprecise_dtypes=True), writes=[cB])
                P.op(POOL, mk("tensor_scalar_min", invcnt[64 * hh:64 * hh + 64, c, :], invcnt[64 * hh:64 * hh + 64, c, :], wv), reads=cA, writes=[cB])
        P.op(DVE, mk("reciprocal", invcnt[:], invcnt[:]), reads=cA, writes=[cB])

        biasT, biasT_b = r1(3088, 1024, BF16)
        stage, stage_b = r1(6944, 32, BF16)

        def build_stage_const(blk):
            sv = stage.rearrange("p (a n) -> p a n", a=2)
            P.op(DVE, mk("memset", stage, 0.0), writes=stage_b)
            if blk < 7:
                P.op(DVE, mk("memset", sv[:, :, blk + 1:8], -BIG), writes=stage_b)

        def stage_to_biasT(t):
            bk, bb = banks[7][:].bitcast(BF16), bank_b[7]
            P.op(PE, mk("transpose", bk[0:64, 0:128], stage, ident[:]), reads=stage_b + cA, writes=[bb])
            P.op(DVE, mk("tensor_copy", biasT[0:64, t * 128:(t + 1) * 128], bk[0:64, 0:128]), reads=[bb], writes=biasT_b)


        def rmsnorm_to(gcol0, sink):
            for g in range(NG):
                gs = slice(g * 512, (g + 1) * 512)
                bk, bb = banks[g % 2], bank_b[g % 2]
                for c in range(8):
                    sq, sqb = r1(256 * (c % 2), 256, BF16)
                    P.op(ACT, mk("activation", out=sq, in_=xT[:, c, gs], func=AF.Square), reads=[xT_b[c][g]], writes=sqb)
                    P.op(PE, mk("matmul", bk[:], onesb[:], sq, start=(c == 0), stop=(c == 7)), reads=sqb + cA, writes=[bb])
                lr, lrb = r1(512 + 512 * (g % 2), 512)
                P.op(ACT, mk("activation", out=lr, in_=bk[:], func=AF.Ln, bias=EPS, scale=1.0 / D), reads=[bb], writes=lrb)
                P.op(ACT, mk("activation", out=lr, in_=lr, func=AF.Exp, scale=-0.5), reads=lrb, writes=lrb)
                for c in range(8):
                    sink(c, g, gs, lr, lrb)

        def norm_to_hT(gcol0):
            def sink(c, g, gs, lr, lrb):
                P.op(DVE, mk("scalar_tensor_tensor", out=hT[:, c, gs], in0=xT[:, c, gs], scalar=smf[:, gcol0 + c:gcol0 + c + 1], in1=lr, op0=ALU.mult, op1=ALU.mult),
                     reads=[xT_b[c][g]] + cA + lrb, writes=[hT_b[c][g]])
            rmsnorm_to(gcol0, sink)

        pstate = {"i": 0}

        def proj_fm(wv, wb, col0, g, bankpool):
            i = pstate["i"]
            pstate["i"] += 1
            bi = bankpool[i % len(bankpool)]
            gs = slice(g * 512, (g + 1) * 512)
            for k in range(8):
                P.op(PE, mk("matmul", banks[bi][:], wv[:, k, col0:col0 + 128], hT[:, k, gs], start=(k == 0), stop=(k == 7)),
                     reads=[wb, hT_b[k][g]], writes=[bank_b[bi]])
            return banks[bi], bank_b[bi]

        def proj_tm(wv, wb, col0, ncols, t, bankpool):
            i = pstate["i"]
            pstate["i"] += 1
            bi = bankpool[i % len(bankpool)]
            for k in range(8):
                P.op(PE, mk("matmul", banks[bi][:, 0:ncols], hT[:, k, t * 128:(t + 1) * 128], wv[:, k, col0:col0 + ncols], start=(k == 0), stop=(k == 7)),
                     reads=[wb, hT_b[k][t // 4]], writes=[bank_b[bi]])
            return banks[bi], bank_b[bi]

        dsems = {}

        def dump(name, ap, rb):
            if name in dbg_d and name not in dsems and not os.environ.get("NODUMP"):
                dsems[name] = P.dma_sem("dsem_" + name)
                P.dma(POOL, mk("dma_start", out=dbg_d[name], in_=ap), dsems[name], reads=rb)

        lcur = [0]

        def ckpt(name):
            if stop == name or stop == "%s@%d" % (name, lcur[0]):
                raise StopBuild()

        try:
          for l in range(L):
              lcur[0] = l
              import os
              if l > 0 and os.environ.get("EPOCH", "1") == "1":
                  P.new_epoch()
              so = SMF_PER * l
              P.dma(SP, mk("dma_start", out=rowp[:], in_=rowp_d[:, ROW_PER * l:ROW_PER * (l + 1)]), rsem, writes=[cR])
              ro = 0
              norm_to_hT(so)

              ckpt("A")
              QT, QT_b = r1(0, 1024, BF16)
              KT, KT_b = r1(1024, 1024, BF16)
              VA, VA_b = r1(2048, 1040, BF16)
              VAv = VA.rearrange("p (t h e) -> p t h e", t=NT, h=2)
              ytok, ytok_b = r1(5136, 1024, BF16)
              ytokv = ytok.rearrange("p (t e) -> p t e", t=NT)
              kmT, kmT_b = r1(6928, 4, BF16)
              gate, gate_b = r1(6976, 16)
              kmf, kmf_b = r1(6992, 8)
              cmp_, cmp_b = r1(7000, 128)
              rank, rank_b = r1(7128, 16)
              rden, rden_b = r1(7144, 4)
              P.op(DVE, mk("memset", VAv[:, :, :, 64:65], 1.0), writes=VA_b)
              for t in range(8):
                  if t % 2 == 0:
                      build_stage_const(t // 2)
                  stage_to_biasT(t)
              for j in range(3):
                  wv, wb = w_get(l, "A%d" % j)
                  for g in range(NG):
                      gs = slice(g * 512, (g + 1) * 512)
                      bk, bb = proj_fm(wv, wb, 0, g, [0, 1, 2])
                      P.op(ACT, mk("activation", out=KT[:, gs], in_=bk[:], func=AF.Copy), reads=[bb], writes=KT_b[4 * g:4 * g + 4])
                  for g in range(NG):
                      gs = slice(g * 512, (g + 1) * 512)
                      bk, bb = proj_fm(wv, wb, 128, g, [0, 1, 2])
                      P.op(ACT, mk("activation", out=QT[:, gs], in_=bk[:], func=AF.Copy), reads=[bb], writes=QT_b[4 * g:4 * g + 4])
                  vstate = [0]

                  def vproj(n):
                      for _ in range(n):
                          t_ = vstate[0]
                          if t_ >= NT:
                              return
                          bk, bb = proj_tm(wv, wb, 256, 128, t_, [0, 1, 2])
                          P.op(DVE, mk("tensor_copy", VAv[:, t_, :, 0:64], bk[:, 0:128].rearrange("p (h e) -> p h e", h=2)), reads=[bb], writes=VA_b)
                          vstate[0] += 1
                  P.op(DVE, mk("tensor_reduce", out=kmf, in_=KT.rearrange("p (n s) -> p n s", n=8), axis=AX.X, op=ALU.add), reads=KT_b, writes=kmf_b)
                  P.op(DVE, mk("tensor_copy", kmT, kmf), reads=kmf_b, writes=kmT_b)
                  for t in range(8, NT):
                      blk = t // 2
                      gk, gb = banks[7], bank_b[7]
                      for a in range(2):
                          P.op(PE, mk("matmul", gk[:, 8 * a:8 * a + blk], QT[64 * a:64 * a + 64, t * 128:(t + 1) * 128], kmT[64 * a:64 * a + 64, 0:blk], start=True, stop=True),
                               reads=QT_b + kmT_b, writes=[gb])
                      gv = gate.rearrange("p (a n) -> p a n", a=2)
                      P.op(DVE, mk("tensor_copy", gv[:, :, 0:blk], gk[:, 0:16].rearrange("p (a n) -> p a n", a=2)[:, :, 0:blk]), reads=[gb], writes=gate_b)
                      cv = cmp_.rearrange("p (a n m) -> p a n m", a=2, n=8)[:, :, 0:blk, 0:blk]
                      P.op(DVE, mk("tensor_tensor", out=cv, in0=gv[:, :, 0:blk].unsqueeze(2).to_broadcast([128, 2, blk, blk]),
                                   in1=gv[:, :, 0:blk].unsqueeze(3).to_broadcast([128, 2, blk, blk]), op=ALU.is_gt), reads=gate_b, writes=cmp_b)
                      rv = rank.rearrange("p (a n) -> p a n", a=2)
                      P.op(DVE, mk("tensor_reduce", out=rv[:, :, 0:blk], in_=cv, axis=AX.X, op=ALU.add), reads=cmp_b, writes=rank_b)
                      P.op(DVE, mk("tensor_scalar", out=rv[:, :, 0:blk], in0=rv[:, :, 0:blk], scalar1=2.5, scalar2=BIG, op0=ALU.is_lt, op1=ALU.mult), reads=rank_b, writes=rank_b)
                      sv = stage.rearrange("p (a n) -> p a n", a=2)
                      if t % 2 == 0:
                          build_stage_const(blk)
                      P.op(DVE, mk("tensor_scalar_add", sv[:, :, 0:blk], rv[:, :, 0:blk], -BIG), reads=rank_b, writes=stage_b)
                      vproj(2)
                      stage_to_biasT(t)
                  vproj(NT)
                  w_done()
                  if l == 0 and j == 0:
                      dump("KT0", KT, KT_b)
                      dump("QT0", QT, QT_b)
                  items = []
                  for a in range(2):
                      for g in range(NG):
                          for kt in range(4 * (g + 1)):
                              items.append((a, g, kt))
                  import os
                  LOOK = int(os.environ.get("ATT_LOOK", "1"))
                  NSB = LOOK + 1

                  def stage_s(i):
                      a, g, kt = items[i]
                      pr = slice(64 * a, 64 * a + 64)
                      br = slice(32 * a, 32 * a + 8)
                      qlo = max(kt, 4 * g)
                      nq = 4 * g + 4 - qlo
                      qs = slice(qlo * 128, (4 * g + 4) * 128)
                      sk, sbb = banks[5 + (i % NSB)], bank_b[5 + (i % NSB)]
                      P.op(PE, mk("matmul", sk[:, 0:nq * 128], KT[pr, kt * 128:(kt + 1) * 128], QT[pr, qs], start=True, stop=False),
                           reads=KT_b + QT_b, writes=[sbb])
                      P.op(PE, mk("matmul", sk[:, 0:nq * 128], indB[br, kt // 2, :], biasT[br, qs], start=False, stop=True),
                           reads=cA + biasT_b, writes=[sbb])
                      pT, pT_b = r1(6160 + 256 * (i % 3), 256, BF16)
                      P.op(ACT, mk("activation", out=pT[:, 0:nq * 128], in_=sk[:, 0:nq * 128], func=AF.Exp, scale=0.125), reads=[sbb], writes=pT_b)
                      if kt >= 4 * g:
                          P.op(DVE, mk("tensor_tensor", out=pT[:, 0:128], in0=pT[:, 0:128], in1=triLEb[:], op=ALU.mult), reads=pT_b + cA, writes=pT_b)

                  def stage_v(i):
                      a, g, kt = items[i]
                      rnd = a * NG + g
                      nkt = 4 * (g + 1)
                      qlo = max(kt, 4 * g)
                      nq = 4 * g + 4 - qlo
                      ok_, ob = banks[3 + (rnd % 2)], bank_b[3 + (rnd % 2)]
                      okv = ok_[:, 0:260].rearrange("p (q e) -> p q e", q=4)
                      pT, pT_b = r1(6160 + 256 * (i % 3), 256, BF16)
                      for qi in range(nq):
                          qt = qlo + qi
                          P.op(PE, mk("matmul", okv[:, qt - 4 * g, :], pT[:, qi * 128:(qi + 1) * 128], VAv[:, kt, a, :], start=(kt == 0 and qi == 0), stop=(kt == nkt - 1 and qi == nq - 1), skip_group_check=True),
                               reads=pT_b + VA_b, writes=[ob])
                      if kt == nkt - 1:
                          rd, rd_b = r1(7144 + 4 * (rnd % 2), 4)
                          rv4 = rd.rearrange("p (q o) -> p q o", q=4)
                          P.op(DVE, mk("reciprocal", rv4, okv[:, :, 64:65]), reads=[ob], writes=rd_b)
                          P.op(DVE, mk("tensor_tensor", out=ytokv[:, 4 * g:4 * g + 4, 64 * a:64 * a + 64], in0=okv[:, :, 0:64], in1=rv4.to_broadcast([128, 4, 64]), op=ALU.mult),
                               reads=[ob] + rd_b, writes=ytok_b)

                  for i in range(min(LOOK, len(items))):
                      stage_s(i)
                  for i in range(len(items)):
                      if i + LOOK < len(items):
                          stage_s(i + LOOK)
                      stage_v(i)
                  for t in range(NT):
                      bk, bb = banks[7][:].bitcast(BF16), bank_b[7]
                      P.op(PE, mk("transpose", bk[:, 0:128], ytokv[:, t, :], ident[:]), reads=ytok_b + cA, writes=[bb])
                      P.op(ACT, mk("activation", out=R2[:, j, t * 128:(t + 1) * 128], in_=bk[:, 0:128], func=AF.Copy), reads=[bb], writes=[R2_b[j][t]])

              ckpt("attn")
              sz, sz_b = r1(0, 3072, BF16)
              szv = sz.rearrange("p (t e) -> p t e", t=NT)
              dtraw, dtraw_b = r1(3072, 96)
              dtv = dtraw.rearrange("p (t h) -> p t h", t=NT)
              cbase = 3200
              xci = 0
              for name, nch in (("X0", 3), ("X1", 3), ("X2", 1)):
                  wv, wb = w_get(l, name)
                  for cc in range(nch):
                      ci = xci + cc
                      carry = None
                      for g in range(NG):
                          gs = slice(g * 512, (g + 1) * 512)
                          bk, bb = proj_fm(wv, wb, cc * 128, g, [0, 1, 2])
                          xp, xp_b = r1(cbase + 520 * (g % 2), 520)
                          if g == 0:
                              P.op(DVE, mk("memset", xp[:, 0:3], 0.0), writes=xp_b)
                          else:
                              P.op(DVE, mk("tensor_copy", xp[:, 0:3], carry[0][:, 512:515]), reads=carry[1], writes=xp_b)
                          P.op(ACT, mk("activation", out=xp[:, 3:515], in_=bk[:], func=AF.Copy), reads=[bb], writes=xp_b)
                          acc, acc_b = r1(cbase + 1040 + 512 * (g % 2), 512)
                          cw = so + 16
                          ceng = DVE
                          P.op(ACT, mk("activation", out=acc, in_=bk[:], func=AF.Copy, scale=smf[:, cw + 21 + ci:cw + 22 + ci]), reads=[bb] + cA, writes=acc_b)
                          for tap in range(0, 3):
                              P.op(ceng, mk("scalar_tensor_tensor", out=acc, in0=xp[:, tap:tap + 512], scalar=smf[:, cw + 7 * tap + ci:cw + 7 * tap + ci + 1], in1=acc, op0=ALU.mult, op1=ALU.add),
                                   reads=xp_b + acc_b + cA, writes=acc_b)
                          P.op(ACT, mk("activation", out=R2[:, 3 + ci, gs], in_=acc, func=AF.Silu, bias=smf[:, so + 44 + ci:so + 45 + ci]), reads=acc_b + cA, writes=R2_b[3 + ci][4 * g:4 * g + 4])
                          carry = (xp, xp_b)
                  xci += nch
                  w_done()
              ckpt("b2x")
              wv, wb = w_get(l, "ZD")
              for t in range(NT):
                  bk, bb = proj_tm(wv, wb, 0, 390, t, [0, 1, 2])
                  P.op(ACT, mk("activation", out=szv[:, t, :], in_=bk[:, 0:384], func=AF.Silu, bias=zcol[:, 0:1]), reads=[bb] + cA, writes=sz_b)
                  P.op(DVE, mk("tensor_copy", dtv[:, t, :], bk[:, 384:390]), reads=[bb], writes=dtraw_b)
              w_done()

              ckpt("b2")
              o = cbase
              dt_, dt_b = r1(o, 96); o += 128
              dA, dA_b = r1(o, 96); o += 128
              arow, arow_b = r1(o, 8); o += 64
              Rh, Rh_b = r1(o, 384, BF16); o += 384
              Rl, Rl_b = r1(o, 384, BF16); o += 384
              E1s = [r1(o + 384 * i, 384, BF16) for i in range(2)]; o += 768
              Gm, Gm_b = r1(o, 256); o += 256
              MTs = [r1(o + 384 * i, 384, BF16) for i in range(2)]; o += 768
              xdts = [r1(o + 192 * i, 192, BF16) for i in range(2)]; o += 384
              xdds = [r1(o + 192 * i, 192, BF16) for i in range(2)]; o += 384
              xDs = [r1(o, 384)] * 2; o += 384
              Btoks = [r1(o + 128 * i, 128, BF16) for i in range(2)]; o += 256
              yc, yc_b = r1(o, 384); o += 384
              y3s, y3s_b = r1(o, 192, BF16); o += 192
              junk, junk_b = r1(o, 192, BF16); o += 192
              prev, prev_b = r1(o, 384); o += 384
              pbfs = [r1(o + 192 * i, 192, BF16) for i in range(2)]; o += 384
              w2s = [r1(o + 64 * i, 8) for i in range(2)]; o += 128
              eacds = [r1(o + 64 * i, 12) for i in range(2)]; o += 128
              ss, ss_b = r1(o, 4); o += 64
              w2off = o; o += 256
              assert o <= R1W, o
              dtall = dt_.rearrange("p (t h) -> p t h", t=NT)
              dAall = dA.rearrange("p (t h) -> p t h", t=NT)
              P.op(DVE, mk("tensor_tensor", out=dtall, in0=dtv, in1=rowp[:, ro:ro + 6].unsqueeze(1).to_broadcast([128, NT, 6]), op=ALU.add), reads=dtraw_b + cA + [cR], writes=dt_b)
              P.op(ACT, mk("activation", out=dt_, in_=dt_, func=AF.Exp), reads=dt_b, writes=dt_b)
              P.op(ACT, mk("activation", out=dt_, in_=dt_, func=AF.Ln, bias=1.0), reads=dt_b, writes=dt_b)
              P.op(ACT, mk("activation", out=arow[:, 0:6], in_=rowp[:, ro + 6:ro + 12], func=AF.Exp), reads=cA + [cR], writes=arow_b)
              P.op(DVE, mk("scalar_tensor_tensor", out=dAall, in0=dtall, scalar=-1.0, in1=arow[:, 0:6].unsqueeze(1).to_broadcast([128, NT, 6]), op0=ALU.mult, op1=ALU.mult), reads=dt_b + arow_b, writes=dA_b)
              dAh, dAh_b = r1(w2off + 128, 48, BF16)
              dAl, dAl_b = r1(w2off + 192, 48, BF16)
              P.op(DVE, mk("tensor_copy", dAh, dA), reads=dA_b, writes=dAh_b)
              P.op(DVE, mk("tensor_tensor", out=dAl, in0=dA, in1=dAh, op=ALU.subtract), reads=dA_b + dAh_b, writes=dAl_b)
              dAhv = dAh.rearrange("p (t h) -> p t h", t=NT)
              dAlv = dAl.rearrange("p (t h) -> p t h", t=NT)
              P.op(DVE, mk("memset", prev, 0.0), writes=prev_b)
              def ssd_front(t):
                  ts_ = slice(t * 128, (t + 1) * 128)
                  E1, E1_b = E1s[t % 2]
                  MT, MT_b = MTs[t % 2]
                  xdt, xdt_b = xdts[t % 2]
                  xdd, xdd_b = xdds[t % 2]
                  xD, xD_b = xDs[t % 2]
                  Btok, Btok_b = Btoks[t % 2]
                  eacd, eacd_b = eacds[t % 2]
                  w2, w2_b = w2s[t % 2]
                  E1v = E1.rearrange("p (h l) -> p h l", h=6)
                  MTv = MT.rearrange("p (h l) -> p h l", h=6)
                  trk, trb = banks[3][:].bitcast(BF16), bank_b[3]
                  for i in range(5):
                      P.op(PE, mk("transpose", trk[:, i * 128:(i + 1) * 128], R2[:, 3 + i, ts_], ident[:]), reads=[R2_b[3 + i][t]] + cA, writes=[trb])
                  xv = trk[:, 0:384].rearrange("p (h e) -> p h e", h=6)
                  gk, gb = banks[4], bank_b[4]
                  for gg in range(2):
                      P.op(PE, mk("matmul", gk[:, gg * 128:(gg + 1) * 128], R2[:, 6 + gg, ts_], R2[:, 8 + gg, ts_], start=True, stop=True), reads=[R2_b[6 + gg][t], R2_b[8 + gg][t]], writes=[gb])
                  for Rx, Rx_b, dv, dvb in ((Rh, Rh_b, dAhv, dAh_b), (Rl, Rl_b, dAlv, dAl_b)):
                      P.op(DVE, mk("tensor_tensor", out=Rx.rearrange("p (h l) -> p h l", h=6), in0=triLEb[:].unsqueeze(1).to_broadcast([128, 6, 128]), in1=dv[:, t, :].unsqueeze(2).to_broadcast([128, 6, 128]), op=ALU.mult),
                           reads=cA + dvb, writes=Rx_b)
                  d1a, d1b_ = banks[0], banks[1]
                  sm, smb = banks[2][:, 0:12], bank_b[2]
                  for ri, (Rx, Rx_b, dv, dvb) in enumerate(((Rh, Rh_b, dAhv, dAh_b), (Rl, Rl_b, dAlv, dAl_b))):
                      P.op(PE, mk("matmul", d1a[:, 0:384], triGT[:], Rx[:, 0:384], start=(ri == 0), stop=(ri == 1)), reads=cA + Rx_b, writes=[bank_b[0]])
                      P.op(PE, mk("matmul", d1b_[:, 0:384], triGT[:], Rx[:, 384:768], start=(ri == 0), stop=(ri == 1)), reads=cA + Rx_b, writes=[bank_b[1]])
                  for ri, (Rx, Rx_b, dv, dvb) in enumerate(((Rh, Rh_b, dAhv, dAh_b), (Rl, Rl_b, dAlv, dAl_b))):
                      P.op(PE, mk("matmul", sm[:, 0:6], triLEb[:], dv[:, t, :], start=(ri == 0), stop=(ri == 1), skip_group_check=True), reads=cA + dvb, writes=[smb])
                  for ri, (Rx, Rx_b, dv, dvb) in enumerate(((Rh, Rh_b, dAhv, dAh_b), (Rl, Rl_b, dAlv, dAl_b))):
                      P.op(PE, mk("matmul", sm[:, 6:12], onesb[:], dv[:, t, :], start=False, stop=(ri == 1), skip_group_check=True), reads=cA + dvb, writes=[smb])
                  P.op(ACT, mk("activation", out=E1[:, 0:384], in_=d1a[:, 0:384], func=AF.Exp), reads=[bank_b[0]], writes=E1_b)
                  P.op(ACT, mk("activation", out=E1[:, 384:768], in_=d1b_[:, 0:384], func=AF.Exp), reads=[bank_b[1]], writes=E1_b)
                  P.op(ACT, mk("activation", out=eacd, in_=sm[:, 0:12], func=AF.Exp), reads=[smb], writes=eacd_b)
                  P.op(DVE, mk("tensor_tensor", out=xdt.rearrange("p (h e) -> p h e", h=6), in0=xv, in1=dtall[:, t, :].unsqueeze(2).to_broadcast([128, 6, 64]), op=ALU.mult), reads=[trb] + dt_b, writes=xdt_b)
                  P.op(DVE, mk("tensor_tensor", out=xD, in0=trk[:, 0:384], in1=rowp[:, ro + 12:ro + 396], op=ALU.mult), reads=[trb] + cA + [cR], writes=xD_b)
                  P.op(ACT, mk("activation", out=Btok, in_=trk[:, 384:640], func=AF.Copy), reads=[trb], writes=Btok_b)
                  P.op(DVE, mk("tensor_tensor", out=Gm.rearrange("p (g l) -> p g l", g=2), in0=gk[:, 0:256].rearrange("p (g l) -> p g l", g=2), in1=triLE[:].unsqueeze(1).to_broadcast([128, 2, 128]), op=ALU.mult),
                       reads=[gb] + cA, writes=Gm_b)
                  P.op(DVE, mk("tensor_tensor", out=w2[:, 0:6], in0=dtall[:, t, :], in1=E1v[:, :, 127], op=ALU.mult), reads=dt_b + E1_b, writes=w2_b)
                  P.op(DVE, mk("tensor_tensor", out=xdd.rearrange("p (h e) -> p h e", h=6), in0=xv, in1=w2[:, 0:6].unsqueeze(2).to_broadcast([128, 6, 64]), op=ALU.mult), reads=[trb] + w2_b, writes=xdd_b)
                  P.op(DVE, mk("tensor_tensor", out=MT.rearrange("p (g r l) -> p g r l", g=2, r=3), in0=Gm.rearrange("p (g l) -> p g l", g=2).unsqueeze(2).to_broadcast([128, 2, 3, 128]),
                               in1=E1.rearrange("p (g r l) -> p g r l", g=2, r=3), op=ALU.mult), reads=Gm_b + E1_b, writes=MT_b)

              def ssd_back(t):
                  ts_ = slice(t * 128, (t + 1) * 128)
                  E1, E1_b = E1s[t % 2]
                  MT, MT_b = MTs[t % 2]
                  xdt, xdt_b = xdts[t % 2]
                  xdd, xdd_b = xdds[t % 2]
                  xD, xD_b = xDs[t % 2]
                  Btok, Btok_b = Btoks[t % 2]
                  eacd, eacd_b = eacds[t % 2]
                  w2, w2_b = w2s[t % 2]
                  E1v = E1.rearrange("p (h l) -> p h l", h=6)
                  MTv = MT.rearrange("p (h l) -> p h l", h=6)
                  yk, ykb = banks[5], bank_b[5]
                  for h in range(6):
                      P.op(PE, mk("matmul", yk[:, h * 64:(h + 1) * 64], MTv[:, h, :], xdt[:, h * 64:(h + 1) * 64], start=True, stop=True), reads=MT_b + xdt_b, writes=[ykb])
                  yo, yob = banks[6], bank_b[6]
                  if t > 0:
                      pb, pb_b = pbfs[t % 2]
                      for gg in range(2):
                          P.op(PE, mk("matmul", yo[:, gg * 192:(gg + 1) * 192], R2[:, 8 + gg, ts_], pb[:, gg * 192:(gg + 1) * 192], start=True, stop=True), reads=[R2_b[8 + gg][t]] + pb_b, writes=[yob])
                  if t < NT - 1:
                      sk_, skb = banks[7], bank_b[7]
                      for gg in range(2):
                          P.op(PE, mk("matmul", sk_[:, gg * 192:(gg + 1) * 192], Btok[:, gg * 128:(gg + 1) * 128], xdd[:, gg * 192:(gg + 1) * 192], start=True, stop=True), reads=Btok_b + xdd_b, writes=[skb])
                      pv = prev.rearrange("p (h e) -> p h e", h=6)
                      P.op(DVE, mk("tensor_tensor", out=pv, in0=pv, in1=eacd[:, 6:12].unsqueeze(2).to_broadcast([128, 6, 64]), op=ALU.mult), reads=prev_b + eacd_b, writes=prev_b)
                      P.op(DVE, mk("tensor_tensor", out=prev, in0=prev, in1=sk_[:, 0:384], op=ALU.add), reads=prev_b + [skb], writes=prev_b)
                      nb, nb_b = pbfs[(t + 1) % 2]
                      P.op(ACT, mk("activation", out=nb, in_=prev, func=AF.Copy), reads=prev_b, writes=nb_b)
                  ycv = yc.rearrange("p (h e) -> p h e", h=6)
                  if t > 0:
                      P.op(DVE, mk("tensor_tensor", out=ycv, in0=yo[:, 0:384].rearrange("p (h e) -> p h e", h=6), in1=eacd[:, 0:6].unsqueeze(2).to_broadcast([128, 6, 64]), op=ALU.mult), reads=[yob] + eacd_b, writes=yc_b)
                      P.op(DVE, mk("tensor_tensor", out=yc, in0=yc, in1=yk[:, 0:384], op=ALU.add), reads=yc_b + [ykb], writes=yc_b)
                      P.op(DVE, mk("tensor_tensor", out=yc, in0=yc, in1=xD, op=ALU.add), reads=yc_b + xD_b, writes=yc_b)
                  else:
                      P.op(DVE, mk("tensor_tensor", out=yc, in0=xD, in1=yk[:, 0:384], op=ALU.add), reads=xD_b + [ykb], writes=yc_b)
                  P.op(DVE, mk("tensor_tensor", out=yc, in0=yc, in1=szv[:, t, :], op=ALU.mult), reads=yc_b + sz_b, writes=yc_b)
                  for gg in range(2):
                      P.op(ACT, mk("activation", out=junk[:, 0:192], in_=yc[:, gg * 192:(gg + 1) * 192], func=AF.Square, accum_out=ss[:, gg:gg + 1]), reads=yc_b, writes=junk_b + ss_b)
                  P.op(ACT, mk("activation", out=ss[:, 0:2], in_=ss[:, 0:2], func=AF.Ln, bias=EPS, scale=1.0 / 192), reads=ss_b, writes=ss_b)
                  P.op(ACT, mk("activation", out=ss[:, 0:2], in_=ss[:, 0:2], func=AF.Exp, scale=-0.5), reads=ss_b, writes=ss_b)
                  for gg in range(2):
                      P.op(DVE, mk("scalar_tensor_tensor", out=y3s[:, gg * 192:(gg + 1) * 192], in0=yc[:, gg * 192:(gg + 1) * 192], scalar=ss[:, gg:gg + 1], in1=rowp[:, ro + 396 + gg * 192:ro + 396 + (gg + 1) * 192],
                                   op0=ALU.mult, op1=ALU.mult), reads=yc_b + ss_b + cA + [cR], writes=y3s_b)
                  tk2, tb2 = banks[2][:].bitcast(BF16), bank_b[2]
                  for i in range(3):
                      P.op(PE, mk("transpose", tk2[:, i * 128:(i + 1) * 128], y3s[:, i * 128:(i + 1) * 128], ident[:]), reads=y3s_b + cA, writes=[tb2])
                  P.op(ACT, mk("activation", out=R2[:, 3:6, ts_], in_=tk2[:, 0:384].rearrange("p (i e) -> p i e", i=3), func=AF.Copy), reads=[tb2], writes=[R2_b[3][t], R2_b[4][t], R2_b[5][t]])


              if os.environ.get("SSDPIPE", "0") == "1":
                  ssd_front(0)
                  for t in range(NT):
                      if t + 1 < NT:
                          ssd_front(t + 1)
                      ssd_back(t)
              else:
                  for t in range(NT):
                      ssd_front(t)
                      ssd_back(t)

              ckpt("ssd")
              wv, wb = w_get(l, "PP")
              PW = 528
              for c in range(2):
                  nlev = 3 if c == 0 else 5
                  prevl = None
                  for g in range(NG):
                      gs = slice(g * 512, (g + 1) * 512)
                      bk, bb = proj_fm(wv, wb, c * 128, g, [0, 1, 2])
                      lv = [r1(cbase + PW * (2 * i + (g % 2)), PW) for i in range(nlev)]
                      for i, (bf_, bfb) in enumerate(lv):
                          if g == 0:
                              P.op(DVE, mk("memset", bf_[:, 0:16], 0.0), writes=bfb)
                          else:
                              P.op(DVE, mk("tensor_copy", bf_[:, 0:16], prevl[i][0][:, 512:528]), reads=prevl[i][1], writes=bfb)
                      p0, p0_b = lv[0]
                      P.op(ACT, mk("activation", out=p0[:, 16:528], in_=bk[:], func=AF.Copy), reads=[bb], writes=p0_b)
                      for i in range(1, nlev):
                          sh = 2 ** (i - 1)
                          src, srcb = lv[i - 1]
                          dst, dstb = lv[i]
                          P.op(DVE, mk("tensor_tensor", out=dst[:, 16:528], in0=src[:, 16:528], in1=src[:, 16 - sh:528 - sh], op=ALU.add), reads=srcb, writes=dstb)
                      dfb, dfb_b = r1(cbase + PW * 10 + 256 * (g % 2), 256, BF16)
                      tmp, tmp_b = r1(cbase + PW * 10 + 512, 16)
                      for hh in range(2):
                          src, srcb = lv[nlev - 2 + hh]
                          hp = slice(64 * hh, 64 * hh + 64)
                          P.op(DVE, mk("scalar_tensor_tensor", out=dfb[hp, :], in0=src[hp, 16:528], scalar=invw[hp, c:c + 1], in1=p0[hp, 16:528], op0=ALU.mult, op1=ALU.subtract), reads=srcb + p0_b + cA, writes=dfb_b)
                          if g == 0:
                              P.op(DVE, mk("tensor_tensor", out=tmp[hp, :], in0=src[hp, 16:32], in1=invcnt[hp, c, :], op=ALU.mult), reads=srcb + cA, writes=tmp_b)
                              P.op(DVE, mk("tensor_tensor", out=dfb[hp, 0:16], in0=tmp[hp, :], in1=p0[hp, 16:32], op=ALU.subtract), reads=tmp_b + p0_b, writes=dfb_b)
                      mk_, mkb = banks[3 + (g % 2)], bank_b[3 + (g % 2)]
                      P.op(PE, mk("matmul", mk_[:], pwb[:, l, c, :], dfb, start=True, stop=True), reads=cA + dfb_b, writes=[mkb])
                      P.op(ACT, mk("activation", out=R2[:, 6 + c, gs], in_=mk_[:], func=AF.Copy, scale=smf[:, so + 51 + c:so + 52 + c]), reads=[mkb] + cA, writes=R2_b[6 + c][4 * g:4 * g + 4])
                      prevl = lv
              w_done()
              if l == 0:
                  dump("ymixT", R2[:, 0:8, :], [b for s_ in range(8) for b in R2_b[s_]])

              ckpt("pool")
              for i, (a0, a1) in enumerate(OUT_TILES):
                  wv, wb = w_get(l, "O%d" % i)
                  for cc in range((a1 - a0) // 128):
                      c = a0 // 128 + cc
                      for g in range(NG):
                          gs = slice(g * 512, (g + 1) * 512)
                          bi = [0, 1, 2][pstate["i"] % 3]
                          pstate["i"] += 1
                          for k in range(8):
                              P.op(PE, mk("matmul", banks[bi][:], wv[:, k, cc * 128:(cc + 1) * 128], R2[:, k, gs], start=(k == 0), stop=(k == 7)), reads=[wb] + R2_b[k][4 * g:4 * g + 4], writes=[bank_b[bi]])
                          P.op(DVE, mk("tensor_tensor", out=xT[:, c, gs], in0=xT[:, c, gs], in1=banks[bi][:], op=ALU.add), reads=[xT_b[c][g], bank_b[bi]], writes=[xT_b[c][g]])
                  w_done()
              if l == 0:
                  dump("xmid", xT[:], [b for c in range(8) for b in xT_b[c]])

              ckpt("out")
              norm_to_hT(so + 8)

              ckpt("H")
              for pi, part in enumerate(FFN_PARTS):
                  for jl, j in enumerate(part):
                      wv, wb = w_get(l, "G%d" % j)
                      for g in range(NG):
                          gs = slice(g * 512, (g + 1) * 512)
                          gk, gb = proj_fm(wv, wb, 0, g, [0, 1, 2, 3])
                          uk, ub = proj_fm(wv, wb, 128, g, [0, 1, 2, 3])
                          sg, sg_b = r1(256 * (g % 2), 256, BF16)
                          P.op(ACT, mk("activation", out=sg, in_=gk[:], func=AF.Silu, bias=zcol[:, 0:1]), reads=[gb] + cA, writes=sg_b)
                          P.op(DVE, mk("tensor_tensor", out=R2[:, jl, gs], in0=sg, in1=uk[:], op=ALU.mult), reads=sg_b + [ub], writes=R2_b[jl][4 * g:4 * g + 4])
                      w_done()
                  for cp in range(4):
                      wv, wb = w_get(l, "D%d_%d" % (pi, cp))
                      for cc in range(2):
                          c = 2 * cp + cc
                          for g in range(NG):
                              gs = slice(g * 512, (g + 1) * 512)
                              bi = [4, 5, 6, 7][pstate["i"] % 4]
                              pstate["i"] += 1
                              for jl in range(len(part)):
                                  P.op(PE, mk("matmul", banks[bi][:], wv[:, jl, cc * 128:(cc + 1) * 128], R2[:, jl, gs], start=(jl == 0), stop=(jl == len(part) - 1)),
                                       reads=[wb] + R2_b[jl][4 * g:4 * g + 4], writes=[bank_b[bi]])
                              P.op(DVE, mk("tensor_tensor", out=xT[:, c, gs], in0=xT[:, c, gs], in1=banks[bi][:], op=ALU.add), reads=[xT_b[c][g], bank_b[bi]], writes=[xT_b[c][g]])
                      w_done()
              if os.environ.get("FENCE", "0") == "1":
                  P.fence()
              ckpt("ffn")
              if l == 0:
                  dump("xl0", xT[:], [b for c in range(8) for b in xT_b[c]])

        except StopBuild:
            dump("ymixT", R2[:, 0:8, :], [b for s_ in range(8) for b in R2_b[s_]])
            dump("xmid", xT[:], [b for c in range(8) for b in xT_b[c]])

        fo = SMF_PER * L

        def fsink(c, g, gs, lr, lrb):
            ob, ob_b = r1(2048 + 512 * ((c + g) % 3), 512)
            P.op(DVE, mk("scalar_tensor_tensor", out=ob, in0=xT[:, c, gs], scalar=smf[:, fo + c:fo + c + 1], in1=lr, op0=ALU.mult, op1=ALU.mult), reads=[xT_b[c][g]] + cA + lrb, writes=ob_b)
            P.dma(SP, mk("dma_start", out=outT_d[c * 128:(c + 1) * 128, gs], in_=ob), sts[(c + g) % 3], reads=ob_b)

        rmsnorm_to(fo, fsink)
        for ss_ in sts + [st] + list(dsems.values()):
            fin = Buf("fin")
            fin.w = ("dma", ss_, ss_[1])
            P.wait_all(SP, [fin])
        P.emit()
    return nc


def prep_inputs(inputs, L=4):
    f = lambda k: np.asarray(inputs[k], dtype=np.float32)
    wst = pack_weights(f("w_in"), f("w_out"), f("w_gate_up"), f("w_down"), L)
    smf, rowp, pw = pack_small(f("norm_mix"), f("norm_ffn"), f("conv_w"), f("conv_b"), f("pool_scale"), f("norm_final"),
                               f("dt_bias"), f("a_log"), f("d_skip"), f("ssd_norm"), f("pool_w"), L)
    x = f("x")
    maps = []
    for b in range(x.shape[0]):
        maps.append({"xT": np.ascontiguousarray(x[b].T), "wst": wst, "smf": smf, "rowp": rowp, "pw": pw})
    return maps


_CACHE = {}


def kernel(**inputs):
    L = 4
    if L not in _CACHE:
        _CACHE[L] = build_program(L)
    nc = _CACHE[L]
    maps = prep_inputs(inputs, L)
    res = run_bass_kernel_spmd(nc, maps, core_ids=list(range(8)))
    out = np.stack([np.ascontiguousarray(r["outT"].T) for r in res.results], axis=0)
    return out.astype(np.float32)
```

```python
import os
import numpy as np
import concourse.bass as bass
import concourse.mybir as mybir
from concourse.bass_utils import run_bass_kernel_spmd
from contextlib import ExitStack

F32 = mybir.dt.float32
BF16 = mybir.dt.bfloat16
ALU = mybir.AluOpType
AF = mybir.ActivationFunctionType
AX = mybir.AxisListType

PE, ACT, DVE, POOL, SP = "pe", "act", "dve", "pool", "sp"
CENG = (PE, ACT, DVE, POOL)

S = 2048
D = 1024
NT = 16
NG = 4
FFN = 2816
NJ = 22
EPS = 1e-6
BIG = 30000.0
FFN_PARTS = [list(range(0, 8)), list(range(8, 15)), list(range(15, 22))]


class Buf:
    __slots__ = ("w", "rs", "name")

    def __init__(self, name=""):
        self.w = None
        self.rs = {}
        self.name = name


class Prog:
    def __init__(self, nc, es):
        self.nc = nc
        self.es = es
        self.q = {e: [] for e in CENG + (SP,)}
        self.ep = -1
        self.sem = {}
        self.n = {}
        self.sig = {}
        self.new_epoch()
        self.seen = {e: {} for e in CENG + (SP,)}
        self.nd = 0

    def new_epoch(self):
        self.ep += 1
        for e in CENG:
            k = (e, self.ep)
            self.sem[k] = self.es.enter_context(self.nc.semaphore("s_%s_%d" % k))
            self.n[k] = 0
            self.sig[k] = [False]

    def dma_sem(self, name):
        self.nd += 1
        return [self.es.enter_context(self.nc.semaphore(name)), 0, self.nd]

    def _deps(self, eng, reads, writes):
        waits = {}
        seen = self.seen[eng]

        def need(ev):
            if ev is None:
                return
            if ev[0] == "eng":
                e2, val = ev[1], ev[2]
                if e2[0] == eng and eng == PE:
                    return
                key = e2
            else:
                ds, val = ev[1], ev[2]
                key = ("d", ds[2])
            if seen.get(key, 0) >= val:
                return
            if key not in waits or waits[key][2] < val:
                waits[key] = ev

        for b in reads:
            need(b.w)
        for b in writes:
            need(b.w)
            for r in b.rs.values():
                need(r)
        for key, ev in waits.items():
            seen[key] = ev[2]
            if ev[0] == "eng":
                self.sig[ev[1]][ev[2]] = True
        return list(waits.values())

    def op(self, eng, fn, reads=(), writes=()):
        waits = self._deps(eng, reads, writes)
        k = (eng, self.ep)
        self.n[k] += 1
        self.sig[k].append(os.environ.get("LAZY", "1") != "1")
        ev = ("eng", k, self.n[k])
        self.q[eng].append((waits, fn, ev))
        for b in reads:
            b.rs[eng] = ev
        for b in writes:
            b.w = ev
            b.rs = {}

    def dma(self, qeng, fn, dsem, reads=(), writes=()):
        waits = self._deps(qeng, reads, writes)
        dsem[1] += 16
        ev = ("dma", dsem, dsem[1])
        self.q[qeng].append((waits, fn, ev))
        for b in reads:
            b.rs[("d", dsem[2])] = ev
        for b in writes:
            b.w = ev
            b.rs = {}

    def fence(self):
        last = []
        for e in CENG:
            k = (e, self.ep)
            if self.n[k] > 0:
                b = Buf()
                b.w = ("eng", k, self.n[k])
                last.append(b)
        for e in CENG + (SP,):
            self.wait_all(e, last)

    def wait_all(self, eng, blist):
        waits = self._deps(eng, blist, ())
        self.q[eng].append((waits, None, None))

    def emit(self):
        cum = {}
        for e in self.n:
            c = [0] * (self.n[e] + 1)
            for i in range(1, self.n[e] + 1):
                c[i] = c[i - 1] + (1 if self.sig[e][i] else 0)
            cum[e] = c
        self.maxcount = {e: cum[e][-1] for e in cum}
        with self.nc.Block() as block:
            def mkb(e):
                def body(engobj):
                    for waits, fn, ev in self.q[e]:
                        for w in waits:
                            if w[0] == "eng":
                                engobj.wait_ge(self.sem[w[1]], cum[w[1]][w[2]])
                            else:
                                engobj.wait_ge(w[1][0], w[2])
                        if fn is not None:
                            ins = fn(engobj)
                            if ev[0] == "dma":
                                ins.then_inc(ev[1][0], 16)
                            elif self.sig[ev[1]][ev[2]]:
                                ins.then_inc(self.sem[ev[1]], 1)
                return body

            block.tensor(mkb(PE))
            block.scalar(mkb(ACT))
            block.vector(mkb(DVE))
            block.gpsimd(mkb(POOL))
            block.sync(mkb(SP))


def mk(f, *a, **k):
    return lambda e: getattr(e, f)(*a, **k)


IN_OFF = dict(q=0, k=384, v=768, z=1152, xbc=1536, dt=2432, p=2438)


def in_tile_cols():
    tiles = []
    for j in range(3):
        cols = (list(range(IN_OFF["k"] + 128 * j, IN_OFF["k"] + 128 * (j + 1)))
                + list(range(IN_OFF["q"] + 128 * j, IN_OFF["q"] + 128 * (j + 1)))
                + list(range(IN_OFF["v"] + 128 * j, IN_OFF["v"] + 128 * (j + 1))))
        tiles.append(("A%d" % j, cols))
    x0 = IN_OFF["xbc"]
    tiles.append(("X0", list(range(x0, x0 + 384))))
    tiles.append(("X1", list(range(x0 + 384, x0 + 768))))
    tiles.append(("X2", list(range(x0 + 768, x0 + 896))))
    tiles.append(("ZD", list(range(IN_OFF["z"], IN_OFF["z"] + 384)) + list(range(IN_OFF["dt"], IN_OFF["dt"] + 6))))
    tiles.append(("PP", list(range(IN_OFF["p"], IN_OFF["p"] + 256))))
    return tiles


OUT_TILES = [(0, 384), (384, 768), (768, 1024)]


def layer_tile_plan():
    plan = []
    for name, cols in in_tile_cols():
        plan.append((name, 8, 400 if name == "ZD" else len(cols)))
    for i, (a, b) in enumerate(OUT_TILES):
        plan.append(("O%d" % i, 8, b - a))
    for pi, part in enumerate(FFN_PARTS):
        for j in part:
            plan.append(("G%d" % j, 8, 256))
        for cp in range(4):
            plan.append(("D%d_%d" % (pi, cp), len(part), 256))
    return plan


def pack_weights(w_in, w_out, w_gate_up, w_down, L):
    plan = layer_tile_plan()
    chunks = []
    for l in range(L):
        intiles = dict(in_tile_cols())
        for name, K, n in plan:
            if name in intiles:
                W = np.zeros((D, n), np.float32)
                W[:, :len(intiles[name])] = w_in[l][:, intiles[name]]
                t = W.reshape(8, 128, n).transpose(1, 0, 2)
            elif name[0] == "O":
                a, b = OUT_TILES[int(name[1:])]
                t = w_out[l][:, a:b].reshape(8, 128, n).transpose(1, 0, 2)
            elif name[0] == "G":
                j = int(name[1:])
                W = np.concatenate([w_gate_up[l][:, j * 128:(j + 1) * 128],
                                    w_gate_up[l][:, FFN + j * 128:FFN + (j + 1) * 128]], axis=1)
                t = W.reshape(8, 128, 256).transpose(1, 0, 2)
            else:
                pi, cp = [int(v) for v in name[1:].split("_")]
                part = FFN_PARTS[pi]
                W = w_down[l][part[0] * 128:(part[-1] + 1) * 128, cp * 256:(cp + 1) * 256]
                t = W.reshape(len(part), 128, 256).transpose(1, 0, 2)
            chunks.append(np.ascontiguousarray(t).reshape(128, K * n))
    return np.ascontiguousarray(np.concatenate(chunks, axis=1), dtype=np.float32)


SMF_PER = 8 + 8 + 28 + 7 + 2
ROW_PER = 6 + 6 + 384 + 384


def pack_small(norm_mix, norm_ffn, conv_w, conv_b, pool_scale, norm_final, dt_bias, a_log, d_skip, ssd_norm, pool_w, L):
    smf = np.zeros((128, SMF_PER * L + 8), np.float32)
    rowp = np.zeros((128, ROW_PER * L), np.float32)
    pw = np.zeros((L, 128, 2, 128), np.float32)
    for l in range(L):
        o = SMF_PER * l
        smf[:, o:o + 8] = norm_mix[l].reshape(8, 128).T
        smf[:, o + 8:o + 16] = norm_ffn[l].reshape(8, 128).T
        for j in range(4):
            smf[:, o + 16 + 7 * j:o + 16 + 7 * (j + 1)] = conv_w[l][j].reshape(7, 128).T
        smf[:, o + 44:o + 51] = conv_b[l].reshape(7, 128).T
        smf[:, o + 51:o + 53] = pool_scale[l].reshape(2, 128).T
        r = ROW_PER * l
        rowp[:, r:r + 6] = dt_bias[l][None, :]
        rowp[:, r + 6:r + 12] = a_log[l][None, :]
        rowp[:, r + 12:r + 396] = np.repeat(d_skip[l], 64)[None, :]
        rowp[:, r + 396:r + 780] = ssd_norm[l][None, :]
        for c in range(2):
            for gg in range(2):
                pw[l, gg * 64:(gg + 1) * 64, c, gg * 64:(gg + 1) * 64] = pool_w[l][2 * c + gg]
    smf[:, SMF_PER * L:SMF_PER * L + 8] = norm_final.reshape(8, 128).T
    return smf, rowp, pw


class StopBuild(Exception):
    pass


def build_program(L=4, dbg=(), stop=None):
    nc = bass.Bass("TRN2", target_bir_lowering=False)
    plan = layer_tile_plan()
    lay_words = sum(K * n for _, K, n in plan)
    xT_d = nc.dram_tensor("xT", [D, S], F32, kind="ExternalInput").ap()
    wst_d = nc.dram_tensor("wst", [128, lay_words * L], F32, kind="ExternalInput").ap()
    smf_d = nc.dram_tensor("smf", [128, SMF_PER * L + 8], F32, kind="ExternalInput").ap()
    rowp_d = nc.dram_tensor("rowp", [128, ROW_PER * L], F32, kind="ExternalInput").ap()
    pw_d = nc.dram_tensor("pw", [L, 128, 2, 128], F32, kind="ExternalInput").ap()
    outT_d = nc.dram_tensor("outT", [D, S], F32, kind="ExternalOutput").ap()
    dbg_d = {}
    for name, shape in dbg:
        dbg_d[name] = nc.dram_tensor("dbg_" + name, list(shape), F32, kind="ExternalOutput").ap()

    with ExitStack() as es:
        P = Prog(nc, es)

        def sb(name, shape, dt):
            return es.enter_context(nc.sbuf_tensor(name, shape, dt))

        xT = sb("xT_sb", [128, 8, S], F32)
        xT_b = [[Buf() for _ in range(NG)] for _ in range(8)]
        hT = sb("hT_sb", [128, 8, S], BF16)
        hT_b = [[Buf() for _ in range(NG)] for _ in range(8)]
        NSLOT = 3
        SLOTW = 3200
        wsl = [sb("wslot%d" % i, [128, SLOTW], BF16) for i in range(NSLOT)]
        wsl_b = [Buf() for _ in range(NSLOT)]
        wsem = [P.dma_sem("wsem%d" % i) for i in range(NSLOT)]
        R2 = sb("R2", [128, 10, S], BF16)
        R2_b = [[Buf() for _ in range(NT)] for _ in range(10)]
        R1W = 9728
        R1 = sb("R1", [128, R1W], F32)
        GR = 64
        R1_b = [Buf() for _ in range(R1W // GR)]
        smf = sb("smf_sb", [128, SMF_PER * L + 8], F32)
        rowp = sb("rowp_sb", [128, ROW_PER], F32)
        cR = Buf("rowp")
        rsem = P.dma_sem("rsem")
        pwb = sb("pw_sb", [128, L, 2, 128], BF16)
        ident = sb("ident", [128, 128], BF16)
        onesb = sb("onesb", [128, 128], BF16)
        onesf = sb("onesf", [128, 128], F32)
        triLE = sb("triLE", [128, 128], F32)
        triLEb = sb("triLEb", [128, 128], BF16)
        triGT = sb("triGT", [128, 128], BF16)
        indB = sb("indB", [64, 8, 128], BF16)
        invcnt = sb("invcnt", [128, 2, 16], F32)
        invw = sb("invw", [128, 2], F32)
        zcol = sb("zcol", [128, 2], F32)
        cB = Buf("consts")
        cS = Buf("consts_sp")
        cP = Buf("consts_pool_dma")
        cA = [cB, cS, cP]
        ld = P.dma_sem("ld")
        ldp = P.dma_sem("ldp")
        st = P.dma_sem("st")
        sts = [P.dma_sem("st%d" % i) for i in range(3)]
        banks = [es.enter_context(nc.psum_tensor("bank%d" % i, [128, 512], F32)) for i in range(8)]
        bank_b = [Buf() for _ in range(8)]

        def r1(offw, nwords, dt=F32):
            ap = R1[:, offw:offw + nwords]
            if dt != F32:
                ap = ap.bitcast(dt)
            return ap, R1_b[offw // GR:(offw + nwords + GR - 1) // GR]

        wq = []
        off = 0
        for l in range(L):
            for name, K, n in plan:
                wq.append((l, name, K, n, off))
                off += K * n
        wstate = {"next": 0, "cur": {}}

        def w_issue():
            i = wstate["next"]
            if i >= len(wq):
                return
            l, name, K, n, off = wq[i]
            s = i % NSLOT
            P.dma(POOL, mk("dma_start", out=wsl[s][:, 0:K * n], in_=wst_d[:, off:off + K * n]), wsem[s], writes=[wsl_b[s]])
            wstate["cur"][(l, name)] = (s, K, n)
            wstate["next"] = i + 1

        def w_get(l, name):
            s, K, n = wstate["cur"][(l, name)]
            return wsl[s][:, 0:K * n].rearrange("p (k n) -> p k n", k=K), wsl_b[s]

        def w_done():
            w_issue()

        for c in range(8):
            P.dma(SP, mk("dma_start", out=xT[:, c, :], in_=xT_d[c * 128:(c + 1) * 128, :]), ld, writes=xT_b[c])
        P.dma(SP, mk("dma_start", out=smf[:], in_=smf_d), ld, writes=[cS])
        for l in range(L):
            P.dma(POOL, mk("dma_start", out=pwb[:, l, :, :], in_=pw_d[l]), ldp, writes=[cP])
        for _ in range(NSLOT):
            w_issue()
        for bb_ in [cS] + [b for c in range(8) for b in xT_b[c]]:
            bb_.w = ("dma", ld, ld[1])
        cP.w = ("dma", ldp, ldp[1])
        P.op(POOL, mk("memset", onesf[:], 1.0), writes=[cB])
        P.op(POOL, mk("memset", zcol[:], 0.0), writes=[cB])
        P.op(POOL, mk("memset", onesb[:], 1.0), writes=[cB])
        P.op(POOL, mk("affine_select", out=ident[:], in_=onesf[:], pattern=[[-1, 128]], compare_op=ALU.is_equal, fill=0.0, base=0, channel_multiplier=1), reads=cA, writes=[cB])
        P.op(POOL, mk("affine_select", out=triLE[:], in_=onesf[:], pattern=[[1, 128]], compare_op=ALU.is_ge, fill=0.0, base=0, channel_multiplier=-1), reads=cA, writes=[cB])
        P.op(POOL, mk("affine_select", out=triLEb[:], in_=onesf[:], pattern=[[1, 128]], compare_op=ALU.is_ge, fill=0.0, base=0, channel_multiplier=-1), reads=cA, writes=[cB])
        P.op(POOL, mk("affine_select", out=triGT[:], in_=onesf[:], pattern=[[-1, 128]], compare_op=ALU.is_gt, fill=0.0, base=0, channel_multiplier=1), reads=cA, writes=[cB])
        P.op(POOL, mk("memset", indB[:], 1.0), writes=[cB])
        for a in range(2):
            P.op(POOL, mk("affine_select", out=indB[32 * a:32 * a + 32, :, :], in_=indB[32 * a:32 * a + 32, :, :],
                             pattern=[[-1, 8], [0, 128]], compare_op=ALU.is_equal, fill=0.0, base=0, channel_multiplier=1), reads=cA, writes=[cB])
        for c in range(2):
            for hh in range(2):
                wv = float(2 ** (2 * c + hh + 1))
                P.op(POOL, mk("memset", invw[64 * hh:64 * hh + 64, c:c + 1], 1.0 / wv), writes=[cB])
                P.op(POOL, mk("iota", invcnt[64 * hh:64 * hh + 64, c, :], [[1, 16]], base=1, channel_multiplier=0, allow_small_or_imprecise_dtypes=True), writes=[cB])
                P.op(POOL, mk("tensor_scalar_min", invcnt[64 * hh:64 * hh + 64, c, :], invcnt[64 * hh:64 * hh + 64, c, :], wv), reads=cA, writes=[cB])
        P.op(DVE, mk("reciprocal", invcnt[:], invcnt[:]), reads=cA, writes=[cB])

        biasT, biasT_b = r1(3088, 1024, BF16)
        stage, stage_b = r1(6944, 32, BF16)

        def build_stage_const(blk):
            sv = stage.rearrange("p (a n) -> p a n", a=2)
            P.op(DVE, mk("memset", stage, 0.0), writes=stage_b)
            if blk < 7:
                P.op(DVE, mk("memset", sv[:, :, blk + 1:8], -BIG), writes=stage_b)

        def stage_to_biasT(t):
            bk, bb = banks[7][:].bitcast(BF16), bank_b[7]
            P.op(PE, mk("transpose", bk[0:64, 0:128], stage, ident[:]), reads=stage_b + cA, writes=[bb])
            P.op(DVE, mk("tensor_copy", biasT[0:64, t * 128:(t + 1) * 128], bk[0:64, 0:128]), reads=[bb], writes=biasT_b)


        def rmsnorm_to(gcol0, sink):
            for g in range(NG):
                gs = slice(g * 512, (g + 1) * 512)
                bk, bb = banks[g % 2], bank_b[g % 2]
                for c in range(8):
                    sq, sqb = r1(256 * (c % 2), 256, BF16)
                    P.op(ACT, mk("activation", out=sq, in_=xT[:, c, gs], func=AF.Square), reads=[xT_b[c][g]], writes=sqb)
                    P.op(PE, mk("matmul", bk[:], onesb[:], sq, start=(c == 0), stop=(c == 7)), reads=sqb + cA, writes=[bb])
                lr, lrb = r1(512 + 512 * (g % 2), 512)
                P.op(ACT, mk("activation", out=lr, in_=bk[:], func=AF.Ln, bias=EPS, scale=1.0 / D), reads=[bb], writes=lrb)
                P.op(ACT, mk("activation", out=lr, in_=lr, func=AF.Exp, scale=-0.5), reads=lrb, writes=lrb)
                for c in range(8):
                    sink(c, g, gs, lr, lrb)

        def norm_to_hT(gcol0):
            def sink(c, g, gs, lr, lrb):
                P.op(DVE, mk("scalar_tensor_tensor", out=hT[:, c, gs], in0=xT[:, c, gs], scalar=smf[:, gcol0 + c:gcol0 + c + 1], in1=lr, op0=ALU.mult, op1=ALU.mult),
                     reads=[xT_b[c][g]] + cA + lrb, writes=[hT_b[c][g]])
            rmsnorm_to(gcol0, sink)

        pstate = {"i": 0}

        def proj_fm(wv, wb, col0, g, bankpool):
            i = pstate["i"]
            pstate["i"] += 1
            bi = bankpool[i % len(bankpool)]
            gs = slice(g * 512, (g + 1) * 512)
            for k in range(8):
                P.op(PE, mk("matmul", banks[bi][:], wv[:, k, col0:col0 + 128], hT[:, k, gs], start=(k == 0), stop=(k == 7)),
                     reads=[wb, hT_b[k][g]], writes=[bank_b[bi]])
            return banks[bi], bank_b[bi]

        def proj_tm(wv, wb, col0, ncols, t, bankpool):
            i = pstate["i"]
            pstate["i"] += 1
            bi = bankpool[i % len(bankpool)]
            for k in range(8):
                P.op(PE, mk("matmul", banks[bi][:, 0:ncols], hT[:, k, t * 128:(t + 1) * 128], wv[:, k, col0:col0 + ncols], start=(k == 0), stop=(k == 7)),
                     reads=[wb, hT_b[k][t // 4]], writes=[bank_b[bi]])
            return banks[bi], bank_b[bi]

        dsems = {}

        def dump(name, ap, rb):
            if name in dbg_d and name not in dsems and not os.environ.get("NODUMP"):
                dsems[name] = P.dma_sem("dsem_" + name)
                P.dma(POOL, mk("dma_start", out=dbg_d[name], in_=ap), dsems[name], reads=rb)

        lcur = [0]

        def ckpt(name):
            if stop == name or stop == "%s@%d" % (name, lcur[0]):
                raise StopBuild()

        try:
          for l in range(L):
              lcur[0] = l
              import os
              if l > 0 and os.environ.get("EPOCH", "1") == "1":
                  P.new_epoch()
              so = SMF_PER * l
              P.dma(SP, mk("dma_start", out=rowp[:], in_=rowp_d[:, ROW_PER * l:ROW_PER * (l + 1)]), rsem, writes=[cR])
              ro = 0
              norm_to_hT(so)

              ckpt("A")
              QT, QT_b = r1(0, 1024, BF16)
              KT, KT_b = r1(1024, 1024, BF16)
              VA, VA_b = r1(2048, 1040, BF16)
              VAv = VA.rearrange("p (t h e) -> p t h e", t=NT, h=2)
              ytok, ytok_b = r1(5136, 1024, BF16)
              ytokv = ytok.rearrange("p (t e) -> p t e", t=NT)
              kmT, kmT_b = r1(6928, 4, BF16)
              gate, gate_b = r1(6976, 16)
              kmf, kmf_b = r1(6992, 8)
              cmp_, cmp_b = r1(7000, 128)
              rank, rank_b = r1(7128, 16)
              rden, rden_b = r1(7144, 4)
              P.op(DVE, mk("memset", VAv[:, :, :, 64:65], 1.0), writes=VA_b)
              for t in range(8):
                  if t % 2 == 0:
                      build_stage_const(t // 2)
                  stage_to_biasT(t)
              for j in range(3):
                  wv, wb = w_get(l, "A%d" % j)
                  for g in range(NG):
                      gs = slice(g * 512, (g + 1) * 512)
                      bk, bb = proj_fm(wv, wb, 0, g, [0, 1, 2])
                      P.op(ACT, mk("activation", out=KT[:, gs], in_=bk[:], func=AF.Copy), reads=[bb], writes=KT_b[4 * g:4 * g + 4])
                  for g in range(NG):
                      gs = slice(g * 512, (g + 1) * 512)
                      bk, bb = proj_fm(wv, wb, 128, g, [0, 1, 2])
                      P.op(ACT, mk("activation", out=QT[:, gs], in_=bk[:], func=AF.Copy), reads=[bb], writes=QT_b[4 * g:4 * g + 4])
                  vstate = [0]

                  def vproj(n):
                      for _ in range(n):
                          t_ = vstate[0]
                          if t_ >= NT:
                              return
                          bk, bb = proj_tm(wv, wb, 256, 128, t_, [0, 1, 2])
                          P.op(DVE, mk("tensor_copy", VAv[:, t_, :, 0:64], bk[:, 0:128].rearrange("p (h e) -> p h e", h=2)), reads=[bb], writes=VA_b)
                          vstate[0] += 1
                  P.op(DVE, mk("tensor_reduce", out=kmf, in_=KT.rearrange("p (n s) -> p n s", n=8), axis=AX.X, op=ALU.add), reads=KT_b, writes=kmf_b)
                  P.op(DVE, mk("tensor_copy", kmT, kmf), reads=kmf_b, writes=kmT_b)
                  for t in range(8, NT):
                      blk = t // 2
                      gk, gb = banks[7], bank_b[7]
                      for a in range(2):
                          P.op(PE, mk("matmul", gk[:, 8 * a:8 * a + blk], QT[64 * a:64 * a + 64, t * 128:(t + 1) * 128], kmT[64 * a:64 * a + 64, 0:blk], start=True, stop=True),
                               reads=QT_b + kmT_b, writes=[gb])
                      gv = gate.rearrange("p (a n) -> p a n", a=2)
                      P.op(DVE, mk("tensor_copy", gv[:, :, 0:blk], gk[:, 0:16].rearrange("p (a n) -> p a n", a=2)[:, :, 0:blk]), reads=[gb], writes=gate_b)
                      cv = cmp_.rearrange("p (a n m) -> p a n m", a=2, n=8)[:, :, 0:blk, 0:blk]
                      P.op(DVE, mk("tensor_tensor", out=cv, in0=gv[:, :, 0:blk].unsqueeze(2).to_broadcast([128, 2, blk, blk]),
                                   in1=gv[:, :, 0:blk].unsqueeze(3).to_broadcast([128, 2, blk, blk]), op=ALU.is_gt), reads=gate_b, writes=cmp_b)
                      rv = rank.rearrange("p (a n) -> p a n", a=2)
                      P.op(DVE, mk("tensor_reduce", out=rv[:, :, 0:blk], in_=cv, axis=AX.X, op=ALU.add), reads=cmp_b, writes=rank_b)
                      P.op(DVE, mk("tensor_scalar", out=rv[:, :, 0:blk], in0=rv[:, :, 0:blk], scalar1=2.5, scalar2=BIG, op0=ALU.is_lt, op1=ALU.mult), reads=rank_b, writes=rank_b)
                      sv = stage.rearrange("p (a n) -> p a n", a=2)
                      if t % 2 == 0:
                          build_stage_const(blk)
                      P.op(DVE, mk("tensor_scalar_add", sv[:, :, 0:blk], rv[:, :, 0:blk], -BIG), reads=rank_b, writes=stage_b)
                      vproj(2)
                      stage_to_biasT(t)
                  vproj(NT)
                  w_done()
                  if l == 0 and j == 0:
                      dump("KT0", KT, KT_b)
                      dump("QT0", QT, QT_b)
                  items = []
                  for a in range(2):
                      for g in range(NG):
                          for kt in range(4 * (g + 1)):
                              items.append((a, g, kt))
                  import os
                  LOOK = int(os.environ.get("ATT_LOOK", "1"))
                  NSB = LOOK + 1

                  def stage_s(i):
                      a, g, kt = items[i]
                      pr = slice(64 * a, 64 * a + 64)
                      br = slice(32 * a, 32 * a + 8)
                      qlo = max(kt, 4 * g)
                      nq = 4 * g + 4 - qlo
                      qs = slice(qlo * 128, (4 * g + 4) * 128)
                      sk, sbb = banks[5 + (i % NSB)], bank_b[5 + (i % NSB)]
                      P.op(PE, mk("matmul", sk[:, 0:nq * 128], KT[pr, kt * 128:(kt + 1) * 128], QT[pr, qs], start=True, stop=False),
                           reads=KT_b + QT_b, writes=[sbb])
                      P.op(PE, mk("matmul", sk[:, 0:nq * 128], indB[br, kt // 2, :], biasT[br, qs], start=False, stop=True),
                           reads=cA + biasT_b, writes=[sbb])
                      pT, pT_b = r1(6160 + 256 * (i % 3), 256, BF16)
                      P.op(ACT, mk("activation", out=pT[:, 0:nq * 128], in_=sk[:, 0:nq * 128], func=AF.Exp, scale=0.125), reads=[sbb], writes=pT_b)
                      if kt >= 4 * g:
                          P.op(DVE, mk("tensor_tensor", out=pT[:, 0:128], in0=pT[:, 0:128], in1=triLEb[:], op=ALU.mult), reads=pT_b + cA, writes=pT_b)

                  def stage_v(i):
                      a, g, kt = items[i]
                      rnd = a * NG + g
                      nkt = 4 * (g + 1)
                      qlo = max(kt, 4 * g)
                      nq = 4 * g + 4 - qlo
                      ok_, ob = banks[3 + (rnd % 2)], bank_b[3 + (rnd % 2)]
                      okv = ok_[:, 0:260].rearrange("p (q e) -> p q e", q=4)
                      pT, pT_b = r1(6160 + 256 * (i % 3), 256, BF16)
                      for qi in range(nq):
                          qt = qlo + qi
                          P.op(PE, mk("matmul", okv[:, qt - 4 * g, :], pT[:, qi * 128:(qi + 1) * 128], VAv[:, kt, a, :], start=(kt == 0 and qi == 0), stop=(kt == nkt - 1 and qi == nq - 1), skip_group_check=True),
                               reads=pT_b + VA_b, writes=[ob])
                      if kt == nkt - 1:
                          rd, rd_b = r1(7144 + 4 * (rnd % 2), 4)
                          rv4 = rd.rearrange("p (q o) -> p q o", q=4)
                          P.op(DVE, mk("reciprocal", rv4, okv[:, :, 64:65]), reads=[ob], writes=rd_b)
                          P.op(DVE, mk("tensor_tensor", out=ytokv[:, 4 * g:4 * g + 4, 64 * a:64 * a + 64], in0=okv[:, :, 0:64], in1=rv4.to_broadcast([128, 4, 64]), op=ALU.mult),
                               reads=[ob] + rd_b, writes=ytok_b)

                  for i in range(min(LOOK, len(items))):
                      stage_s(i)
                  for i in range(len(items)):
                      if i + LOOK < len(items):
                          stage_s(i + LOOK)
                      stage_v(i)
                  for t in range(NT):
                      bk, bb = banks[7][:].bitcast(BF16), bank_b[7]
                      P.op(PE, mk("transpose", bk[:, 0:128], ytokv[:, t, :], ident[:]), reads=ytok_b + cA, writes=[bb])
                      P.op(ACT, mk("activation", out=R2[:, j, t * 128:(t + 1) * 128], in_=bk[:, 0:128], func=AF.Copy), reads=[bb], writes=[R2_b[j][t]])

              ckpt("attn")
              sz, sz_b = r1(0, 3072, BF16)
              szv = sz.rearrange("p (t e) -> p t e", t=NT)
              dtraw, dtraw_b = r1(3072, 96)
              dtv = dtraw.rearrange("p (t h) -> p t h", t=NT)
              cbase = 3200
              xci = 0
              for name, nch in (("X0", 3), ("X1", 3), ("X2", 1)):
                  wv, wb = w_get(l, name)
                  for cc in range(nch):
                      ci = xci + cc
                      carry = None
                      for g in range(NG):
                          gs = slice(g * 512, (g + 1) * 512)
                          bk, bb = proj_fm(wv, wb, cc * 128, g, [0, 1, 2])
                          xp, xp_b = r1(cbase + 520 * (g % 2), 520)
                          if g == 0:
                              P.op(DVE, mk("memset", xp[:, 0:3], 0.0), writes=xp_b)
                          else:
                              P.op(DVE, mk("tensor_copy", xp[:, 0:3], carry[0][:, 512:515]), reads=carry[1], writes=xp_b)
                          P.op(ACT, mk("activation", out=xp[:, 3:515], in_=bk[:], func=AF.Copy), reads=[bb], writes=xp_b)
                          acc, acc_b = r1(cbase + 1040 + 512 * (g % 2), 512)
                          cw = so + 16
                          ceng = DVE
                          P.op(ACT, mk("activation", out=acc, in_=bk[:], func=AF.Copy, scale=smf[:, cw + 21 + ci:cw + 22 + ci]), reads=[bb] + cA, writes=acc_b)
                          for tap in range(0, 3):
                              P.op(ceng, mk("scalar_tensor_tensor", out=acc, in0=xp[:, tap:tap + 512], scalar=smf[:, cw + 7 * tap + ci:cw + 7 * tap + ci + 1], in1=acc, op0=ALU.mult, op1=ALU.add),
                                   reads=xp_b + acc_b + cA, writes=acc_b)
                          P.op(ACT, mk("activation", out=R2[:, 3 + ci, gs], in_=acc, func=AF.Silu, bias=smf[:, so + 44 + ci:so + 45 + ci]), reads=acc_b + cA, writes=R2_b[3 + ci][4 * g:4 * g + 4])
                          carry = (xp, xp_b)
                  xci += nch
                  w_done()
              ckpt("b2x")
              wv, wb = w_get(l, "ZD")
              for t in range(NT):
                  bk, bb = proj_tm(wv, wb, 0, 390, t, [0, 1, 2])
                  P.op(ACT, mk("activation", out=szv[:, t, :], in_=bk[:, 0:384], func=AF.Silu, bias=zcol[:, 0:1]), reads=[bb] + cA, writes=sz_b)
                  P.op(DVE, mk("tensor_copy", dtv[:, t, :], bk[:, 384:390]), reads=[bb], writes=dtraw_b)
              w_done()

              ckpt("b2")
              o = cbase
              dt_, dt_b = r1(o, 96); o += 128
              dA, dA_b = r1(o, 96); o += 128
              arow, arow_b = r1(o, 8); o += 64
              Rh, Rh_b = r1(o, 384, BF16); o += 384
              Rl, Rl_b = r1(o, 384, BF16); o += 384
              E1s = [r1(o + 384 * i, 384, BF16) for i in range(2)]; o += 768
              Gm, Gm_b = r1(o, 256); o += 256
              MTs = [r1(o + 384 * i, 384, BF16) for i in range(2)]; o += 768
              xdts = [r1(o + 192 * i, 192, BF16) for i in range(2)]; o += 384
              xdds = [r1(o + 192 * i, 192, BF16) for i in range(2)]; o += 384
              xDs = [r1(o, 384)] * 2; o += 384
              Btoks = [r1(o + 128 * i, 128, BF16) for i in range(2)]; o += 256
              yc, yc_b = r1(o, 384); o += 384
              y3s, y3s_b = r1(o, 192, BF16); o += 192
              junk, junk_b = r1(o, 192, BF16); o += 192
              prev, prev_b = r1(o, 384); o += 384
              pbfs = [r1(o + 192 * i, 192, BF16) for i in range(2)]; o += 384
              w2s = [r1(o + 64 * i, 8) for i in range(2)]; o += 128
              eacds = [r1(o + 64 * i, 12) for i in range(2)]; o += 128
              ss, ss_b = r1(o, 4); o += 64
              w2off = o; o += 256
              assert o <= R1W, o
              dtall = dt_.rearrange("p (t h) -> p t h", t=NT)
              dAall = dA.rearrange("p (t h) -> p t h", t=NT)
              P.op(DVE, mk("tensor_tensor", out=dtall, in0=dtv, in1=rowp[:, ro:ro + 6].unsqueeze(1).to_broadcast([128, NT, 6]), op=ALU.add), reads=dtraw_b + cA + [cR], writes=dt_b)
              P.op(ACT, mk("activation", out=dt_, in_=dt_, func=AF.Exp), reads=dt_b, writes=dt_b)
              P.op(ACT, mk("activation", out=dt_, in_=dt_, func=AF.Ln, bias=1.0), reads=dt_b, writes=dt_b)
              P.op(ACT, mk("activation", out=arow[:, 0:6], in_=rowp[:, ro + 6:ro + 12], func=AF.Exp), reads=cA + [cR], writes=arow_b)
              P.op(DVE, mk("scalar_tensor_tensor", out=dAall, in0=dtall, scalar=-1.0, in1=arow[:, 0:6].unsqueeze(1).to_broadcast([128, NT, 6]), op0=ALU.mult, op1=ALU.mult), reads=dt_b + arow_b, writes=dA_b)
              dAh, dAh_b = r1(w2off + 128, 48, BF16)
              dAl, dAl_b = r1(w2off + 192, 48, BF16)
              P.op(DVE, mk("tensor_copy", dAh, dA), reads=dA_b, writes=dAh_b)
              P.op(DVE, mk("tensor_tensor", out=dAl, in0=dA, in1=dAh, op=ALU.subtract), reads=dA_b + dAh_b, writes=dAl_b)
              dAhv = dAh.rearrange("p (t h) -> p t h", t=NT)
              dAlv = dAl.rearrange("p (t h) -> p t h", t=NT)
              P.op(DVE, mk("memset", prev, 0.0), writes=prev_b)
              def ssd_front(t):
                  ts_ = slice(t * 128, (t + 1) * 128)
                  E1, E1_b = E1s[t % 2]
                  MT, MT_b = MTs[t % 2]
                  xdt, xdt_b = xdts[t % 2]
                  xdd, xdd_b = xdds[t % 2]
                  xD, xD_b = xDs[t % 2]
                  Btok, Btok_b = Btoks[t % 2]
                  eacd, eacd_b = eacds[t % 2]
                  w2, w2_b = w2s[t % 2]
                  E1v = E1.rearrange("p (h l) -> p h l", h=6)
                  MTv = MT.rearrange("p (h l) -> p h l", h=6)
                  trk, trb = banks[3][:].bitcast(BF16), bank_b[3]
                  for i in range(5):
                      P.op(PE, mk("transpose", trk[:, i * 128:(i + 1) * 128], R2[:, 3 + i, ts_], ident[:]), reads=[R2_b[3 + i][t]] + cA, writes=[trb])
                  xv = trk[:, 0:384].rearrange("p (h e) -> p h e", h=6)
                  gk, gb = banks[4], bank_b[4]
                  for gg in range(2):
                      P.op(PE, mk("matmul", gk[:, gg * 128:(gg + 1) * 128], R2[:, 6 + gg, ts_], R2[:, 8 + gg, ts_], start=True, stop=True), reads=[R2_b[6 + gg][t], R2_b[8 + gg][t]], writes=[gb])
                  for Rx, Rx_b, dv, dvb in ((Rh, Rh_b, dAhv, dAh_b), (Rl, Rl_b, dAlv, dAl_b)):
                      P.op(DVE, mk("tensor_tensor", out=Rx.rearrange("p (h l) -> p h l", h=6), in0=triLEb[:].unsqueeze(1).to_broadcast([128, 6, 128]), in1=dv[:, t, :].unsqueeze(2).to_broadcast([128, 6, 128]), op=ALU.mult),
                           reads=cA + dvb, writes=Rx_b)
                  d1a, d1b_ = banks[0], banks[1]
                  sm, smb = banks[2][:, 0:12], bank_b[2]
                  for ri, (Rx, Rx_b, dv, dvb) in enumerate(((Rh, Rh_b, dAhv, dAh_b), (Rl, Rl_b, dAlv, dAl_b))):
                      P.op(PE, mk("matmul", d1a[:, 0:384], triGT[:], Rx[:, 0:384], start=(ri == 0), stop=(ri == 1)), reads=cA + Rx_b, writes=[bank_b[0]])
                      P.op(PE, mk("matmul", d1b_[:, 0:384], triGT[:], Rx[:, 384:768], start=(ri == 0), stop=(ri == 1)), reads=cA + Rx_b, writes=[bank_b[1]])
                  for ri, (Rx, Rx_b, dv, dvb) in enumerate(((Rh, Rh_b, dAhv, dAh_b), (Rl, Rl_b, dAlv, dAl_b))):
                      P.op(PE, mk("matmul", sm[:, 0:6], triLEb[:], dv[:, t, :], start=(ri == 0), stop=(ri == 1), skip_group_check=True), reads=cA + dvb, writes=[smb])
                  for ri, (Rx, Rx_b, dv, dvb) in enumerate(((Rh, Rh_b, dAhv, dAh_b), (Rl, Rl_b, dAlv, dAl_b))):
                      P.op(PE, mk("matmul", sm[:, 6:12], onesb[:], dv[:, t, :], start=False, stop=(ri == 1), skip_group_check=True), reads=cA + dvb, writes=[smb])
                  P.op(ACT, mk("activation", out=E1[:, 0:384], in_=d1a[:, 0:384], func=AF.Exp), reads=[bank_b[0]], writes=E1_b)
                  P.op(ACT, mk("activation", out=E1[:, 384:768], in_=d1b_[:, 0:384], func=AF.Exp), reads=[bank_b[1]], writes=E1_b)
                  P.op(ACT, mk("activation", out=eacd, in_=sm[:, 0:12], func=AF.Exp), reads=[smb], writes=eacd_b)
                  P.op(DVE, mk("tensor_tensor", out=xdt.rearrange("p (h e) -> p h e", h=6), in0=xv, in1=dtall[:, t, :].unsqueeze(2).to_broadcast([128, 6, 64]), op=ALU.mult), reads=[trb] + dt_b, writes=xdt_b)
                  P.op(DVE, mk("tensor_tensor", out=xD, in0=trk[:, 0:384], in1=rowp[:, ro + 12:ro + 396], op=ALU.mult), reads=[trb] + cA + [cR], writes=xD_b)
                  P.op(ACT, mk("activation", out=Btok, in_=trk[:, 384:640], func=AF.Copy), reads=[trb], writes=Btok_b)
                  P.op(DVE, mk("tensor_tensor", out=Gm.rearrange("p (g l) -> p g l", g=2), in0=gk[:, 0:256].rearrange("p (g l) -> p g l", g=2), in1=triLE[:].unsqueeze(1).to_broadcast([128, 2, 128]), op=ALU.mult),
                       reads=[gb] + cA, writes=Gm_b)
                  P.op(DVE, mk("tensor_tensor", out=w2[:, 0:6], in0=dtall[:, t, :], in1=E1v[:, :, 127], op=ALU.mult), reads=dt_b + E1_b, writes=w2_b)
                  P.op(DVE, mk("tensor_tensor", out=xdd.rearrange("p (h e) -> p h e", h=6), in0=xv, in1=w2[:, 0:6].unsqueeze(2).to_broadcast([128, 6, 64]), op=ALU.mult), reads=[trb] + w2_b, writes=xdd_b)
                  P.op(DVE, mk("tensor_tensor", out=MT.rearrange("p (g r l) -> p g r l", g=2, r=3), in0=Gm.rearrange("p (g l) -> p g l", g=2).unsqueeze(2).to_broadcast([128, 2, 3, 128]),
                               in1=E1.rearrange("p (g r l) -> p g r l", g=2, r=3), op=ALU.mult), reads=Gm_b + E1_b, writes=MT_b)

              def ssd_back(t):
                  ts_ = slice(t * 128, (t + 1) * 128)
                  E1, E1_b = E1s[t % 2]
                  MT, MT_b = MTs[t % 2]
                  xdt, xdt_b = xdts[t % 2]
                  xdd, xdd_b = xdds[t % 2]
                  xD, xD_b = xDs[t % 2]
                  Btok, Btok_b = Btoks[t % 2]
                  eacd, eacd_b = eacds[t % 2]
                  w2, w2_b = w2s[t % 2]
                  E1v = E1.rearrange("p (h l) -> p h l", h=6)
                  MTv = MT.rearrange("p (h l) -> p h l", h=6)
                  if t < NT - 1:
                      sk_, skb = banks[7], bank_b[7]
                      for gg in range(2):
                          P.op(PE, mk("matmul", sk_[:, gg * 192:(gg + 1) * 192], Btok[:, gg * 128:(gg + 1) * 128], xdd[:, gg * 192:(gg + 1) * 192], start=True, stop=True), reads=Btok_b + xdd_b, writes=[skb])
                      pv = prev.rearrange("p (h e) -> p h e", h=6)
                      P.op(DVE, mk("tensor_tensor", out=pv, in0=pv, in1=eacd[:, 6:12].unsqueeze(2).to_broadcast([128, 6, 64]), op=ALU.mult), reads=prev_b + eacd_b, writes=prev_b)
                      P.op(DVE, mk("tensor_tensor", out=prev, in0=prev, in1=sk_[:, 0:384], op=ALU.add), reads=prev_b + [skb], writes=prev_b)
                      nb, nb_b = pbfs[(t + 1) % 2]
                      P.op(ACT, mk("activation", out=nb, in_=prev, func=AF.Copy), reads=prev_b, writes=nb_b)
                  yk, ykb = banks[5], bank_b[5]
                  for h in range(6):
                      P.op(PE, mk("matmul", yk[:, h * 64:(h + 1) * 64], MTv[:, h, :], xdt[:, h * 64:(h + 1) * 64], start=True, stop=True), reads=MT_b + xdt_b, writes=[ykb])
                  yo, yob = banks[6], bank_b[6]
                  if t > 0:
                      pb, pb_b = pbfs[t % 2]
                      for gg in range(2):
                          P.op(PE, mk("matmul", yo[:, gg * 192:(gg + 1) * 192], R2[:, 8 + gg, ts_], pb[:, gg * 192:(gg + 1) * 192], start=True, stop=True), reads=[R2_b[8 + gg][t]] + pb_b, writes=[yob])
                  ycv = yc.rearrange("p (h e) -> p h e", h=6)
                  if t > 0:
                      P.op(DVE, mk("tensor_tensor", out=ycv, in0=yo[:, 0:384].rearrange("p (h e) -> p h e", h=6), in1=eacd[:, 0:6].unsqueeze(2).to_broadcast([128, 6, 64]), op=ALU.mult), reads=[yob] + eacd_b, writes=yc_b)
                      P.op(DVE, mk("tensor_tensor", out=yc, in0=yc, in1=yk[:, 0:384], op=ALU.add), reads=yc_b + [ykb], writes=yc_b)
                      P.op(DVE, mk("tensor_tensor", out=yc, in0=yc, in1=xD, op=ALU.add), reads=yc_b + xD_b, writes=yc_b)
                  else:
                      P.op(DVE, mk("tensor_tensor", out=yc, in0=xD, in1=yk[:, 0:384], op=ALU.add), reads=xD_b + [ykb], writes=yc_b)
                  P.op(DVE, mk("tensor_tensor", out=yc, in0=yc, in1=szv[:, t, :], op=ALU.mult), reads=yc_b + sz_b, writes=yc_b)
                  for gg in range(2):
                      P.op(ACT, mk("activation", out=junk[:, 0:192], in_=yc[:, gg * 192:(gg + 1) * 192], func=AF.Square, accum_out=ss[:, gg:gg + 1]), reads=yc_b, writes=junk_b + ss_b)
                  P.op(ACT, mk("activation", out=ss[:, 0:2], in_=ss[:, 0:2], func=AF.Ln, bias=EPS, scale=1.0 / 192), reads=ss_b, writes=ss_b)
                  P.op(ACT, mk("activation", out=ss[:, 0:2], in_=ss[:, 0:2], func=AF.Exp, scale=-0.5), reads=ss_b, writes=ss_b)
                  for gg in range(2):
                      P.op(DVE, mk("scalar_tensor_tensor", out=y3s[:, gg * 192:(gg + 1) * 192], in0=yc[:, gg * 192:(gg + 1) * 192], scalar=ss[:, gg:gg + 1], in1=rowp[:, ro + 396 + gg * 192:ro + 396 + (gg + 1) * 192],
                                   op0=ALU.mult, op1=ALU.mult), reads=yc_b + ss_b + cA + [cR], writes=y3s_b)
                  tk2, tb2 = banks[2][:].bitcast(BF16), bank_b[2]
                  for i in range(3):
                      P.op(PE, mk("transpose", tk2[:, i * 128:(i + 1) * 128], y3s[:, i * 128:(i + 1) * 128], ident[:]), reads=y3s_b + cA, writes=[tb2])
                  P.op(ACT, mk("activation", out=R2[:, 3:6, ts_], in_=tk2[:, 0:384].rearrange("p (i e) -> p i e", i=3), func=AF.Copy), reads=[tb2], writes=[R2_b[3][t], R2_b[4][t], R2_b[5][t]])


              if os.environ.get("SSDPIPE", "0") == "1":
                  ssd_front(0)
                  for t in range(NT):
                      if t + 1 < NT:
                          ssd_front(t + 1)
                      ssd_back(t)
              else:
                  for t in range(NT):
                      ssd_front(t)
                      ssd_back(t)

              ckpt("ssd")
              wv, wb = w_get(l, "PP")
              PW = 528
              for c in range(2):
                  nlev = 3 if c == 0 else 5
                  prevl = None
                  for g in range(NG):
                      gs = slice(g * 512, (g + 1) * 512)
                      bk, bb = proj_fm(wv, wb, c * 128, g, [0, 1, 2])
                      lv = [r1(cbase + PW * (2 * i + (g % 2)), PW) for i in range(nlev)]
                      for i, (bf_, bfb) in enumerate(lv):
                          if g == 0:
                              P.op(DVE, mk("memset", bf_[:, 0:16], 0.0), writes=bfb)
                          else:
                              P.op(DVE, mk("tensor_copy", bf_[:, 0:16], prevl[i][0][:, 512:528]), reads=prevl[i][1], writes=bfb)
                      p0, p0_b = lv[0]
                      P.op(ACT, mk("activation", out=p0[:, 16:528], in_=bk[:], func=AF.Copy), reads=[bb], writes=p0_b)
                      for i in range(1, nlev):
                          sh = 2 ** (i - 1)
                          src, srcb = lv[i - 1]
                          dst, dstb = lv[i]
                          P.op(DVE, mk("tensor_tensor", out=dst[:, 16:528], in0=src[:, 16:528], in1=src[:, 16 - sh:528 - sh], op=ALU.add), reads=srcb, writes=dstb)
                      dfb, dfb_b = r1(cbase + PW * 10 + 256 * (g % 2), 256, BF16)
                      tmp, tmp_b = r1(cbase + PW * 10 + 512, 16)
                      for hh in range(2):
                          src, srcb = lv[nlev - 2 + hh]
                          hp = slice(64 * hh, 64 * hh + 64)
                          P.op(DVE, mk("scalar_tensor_tensor", out=dfb[hp, :], in0=src[hp, 16:528], scalar=invw[hp, c:c + 1], in1=p0[hp, 16:528], op0=ALU.mult, op1=ALU.subtract), reads=srcb + p0_b + cA, writes=dfb_b)
                          if g == 0:
                              P.op(DVE, mk("tensor_tensor", out=tmp[hp, :], in0=src[hp, 16:32], in1=invcnt[hp, c, :], op=ALU.mult), reads=srcb + cA, writes=tmp_b)
                              P.op(DVE, mk("tensor_tensor", out=dfb[hp, 0:16], in0=tmp[hp, :], in1=p0[hp, 16:32], op=ALU.subtract), reads=tmp_b + p0_b, writes=dfb_b)
                      mk_, mkb = banks[3 + (g % 2)], bank_b[3 + (g % 2)]
                      P.op(PE, mk("matmul", mk_[:], pwb[:, l, c, :], dfb, start=True, stop=True), reads=cA + dfb_b, writes=[mkb])
                      P.op(ACT, mk("activation", out=R2[:, 6 + c, gs], in_=mk_[:], func=AF.Copy, scale=smf[:, so + 51 + c:so + 52 + c]), reads=[mkb] + cA, writes=R2_b[6 + c][4 * g:4 * g + 4])
                      prevl = lv
              w_done()
              if l == 0:
                  dump("ymixT", R2[:, 0:8, :], [b for s_ in range(8) for b in R2_b[s_]])

              ckpt("pool")
              for i, (a0, a1) in enumerate(OUT_TILES):
                  wv, wb = w_get(l, "O%d" % i)
                  for cc in range((a1 - a0) // 128):
                      c = a0 // 128 + cc
                      for g in range(NG):
                          gs = slice(g * 512, (g + 1) * 512)
                          bi = [0, 1, 2][pstate["i"] % 3]
                          pstate["i"] += 1
                          for k in range(8):
                              P.op(PE, mk("matmul", banks[bi][:], wv[:, k, cc * 128:(cc + 1) * 128], R2[:, k, gs], start=(k == 0), stop=(k == 7)), reads=[wb] + R2_b[k][4 * g:4 * g + 4], writes=[bank_b[bi]])
                          P.op(DVE, mk("tensor_tensor", out=xT[:, c, gs], in0=xT[:, c, gs], in1=banks[bi][:], op=ALU.add), reads=[xT_b[c][g], bank_b[bi]], writes=[xT_b[c][g]])
                  w_done()
              if l == 0:
                  dump("xmid", xT[:], [b for c in range(8) for b in xT_b[c]])

              ckpt("out")
              norm_to_hT(so + 8)

              ckpt("H")
              for pi, part in enumerate(FFN_PARTS):
                  for jl, j in enumerate(part):
                      wv, wb = w_get(l, "G%d" % j)
                      for g in range(NG):
                          gs = slice(g * 512, (g + 1) * 512)
                          gk, gb = proj_fm(wv, wb, 0, g, [0, 1, 2, 3])
                          uk, ub = proj_fm(wv, wb, 128, g, [0, 1, 2, 3])
                          sg, sg_b = r1(256 * (g % 2), 256, BF16)
                          P.op(ACT, mk("activation", out=sg, in_=gk[:], func=AF.Silu, bias=zcol[:, 0:1]), reads=[gb] + cA, writes=sg_b)
                          P.op(DVE, mk("tensor_tensor", out=R2[:, jl, gs], in0=sg, in1=uk[:], op=ALU.mult), reads=sg_b + [ub], writes=R2_b[jl][4 * g:4 * g + 4])
                      w_done()
                  for cp in range(4):
                      wv, wb = w_get(l, "D%d_%d" % (pi, cp))
                      for cc in range(2):
                          c = 2 * cp + cc
                          for g in range(NG):
                              gs = slice(g * 512, (g + 1) * 512)
                              bi = [4, 5, 6, 7][pstate["i"] % 4]
                              pstate["i"] += 1
                              for jl in range(len(part)):
                                  P.op(PE, mk("matmul", banks[bi][:], wv[:, jl, cc * 128:(cc + 1) * 128], R2[:, jl, gs], start=(jl == 0), stop=(jl == len(part) - 1)),
                                       reads=[wb] + R2_b[jl][4 * g:4 * g + 4], writes=[bank_b[bi]])
                              P.op(DVE, mk("tensor_tensor", out=xT[:, c, gs], in0=xT[:, c, gs], in1=banks[bi][:], op=ALU.add), reads=[xT_b[c][g], bank_b[bi]], writes=[xT_b[c][g]])
                      w_done()
              if os.environ.get("FENCE", "0") == "1":
                  P.fence()
              ckpt("ffn")
              if l == 0:
                  dump("xl0", xT[:], [b for c in range(8) for b in xT_b[c]])

        except StopBuild:
            dump("ymixT", R2[:, 0:8, :], [b for s_ in range(8) for b in R2_b[s_]])
            dump("xmid", xT[:], [b for c in range(8) for b in xT_b[c]])

        fo = SMF_PER * L

        def fsink(c, g, gs, lr, lrb):
            ob, ob_b = r1(2048 + 512 * ((c + g) % 3), 512)
            P.op(DVE, mk("scalar_tensor_tensor", out=ob, in0=xT[:, c, gs], scalar=smf[:, fo + c:fo + c + 1], in1=lr, op0=ALU.mult, op1=ALU.mult), reads=[xT_b[c][g]] + cA + lrb, writes=ob_b)
            P.dma(SP, mk("dma_start", out=outT_d[c * 128:(c + 1) * 128, gs], in_=ob), sts[(c + g) % 3], reads=ob_b)

        rmsnorm_to(fo, fsink)
        for ss_ in sts + [st] + list(dsems.values()):
            fin = Buf("fin")
            fin.w = ("dma", ss_, ss_[1])
            P.wait_all(SP, [fin])
        P.emit()
    return nc


def prep_inputs(inputs, L=4):
    f = lambda k: np.asarray(inputs[k], dtype=np.float32)
    wst = pack_weights(f("w_in"), f("w_out"), f("w_gate_up"), f("w_down"), L)
    smf, rowp, pw = pack_small(f("norm_mix"), f("norm_ffn"), f("conv_w"), f("conv_b"), f("pool_scale"), f("norm_final"),
                               f("dt_bias"), f("a_log"), f("d_skip"), f("ssd_norm"), f("pool_w"), L)
    x = f("x")
    maps = []
    for b in range(x.shape[0]):
        maps.append({"xT": np.ascontiguousarray(x[b].T), "wst": wst, "smf": smf, "rowp": rowp, "pw": pw})
    return maps


_CACHE = {}


def kernel(**inputs):
    L = 4
    if L not in _CACHE:
        _CACHE[L] = build_program(L)
    nc = _CACHE[L]
    maps = prep_inputs(inputs, L)
    res = run_bass_kernel_spmd(nc, maps, core_ids=list(range(8)))
    out = np.stack([np.ascontiguousarray(r["outT"].T) for r in res.results], axis=0)
    return out.astype(np.float32)
```

```python
import os
import numpy as np
import concourse.bass as bass
import concourse.mybir as mybir
from concourse.bass_utils import run_bass_kernel_spmd
from contextlib import ExitStack

F32 = mybir.dt.float32
BF16 = mybir.dt.bfloat16
ALU = mybir.AluOpType
AF = mybir.ActivationFunctionType
AX = mybir.AxisListType

PE, ACT, DVE, POOL, SP = "pe", "act", "dve", "pool", "sp"
CENG = (PE, ACT, DVE, POOL)

S = 2048
D = 1024
NT = 16
NG = 4
FFN = 2816
NJ = 22
EPS = 1e-6
BIG = 30000.0
FFN_PARTS = [list(range(0, 8)), list(range(8, 15)), list(range(15, 22))]


class Buf:
    __slots__ = ("w", "rs", "name")

    def __init__(self, name=""):
        self.w = None
        self.rs = {}
        self.name = name


class Prog:
    def __init__(self, nc, es):
        self.nc = nc
        self.es = es
        self.q = {e: [] for e in CENG + (SP,)}
        self.ep = -1
        self.sem = {}
        self.n = {}
        self.sig = {}
        self.new_epoch()
        self.seen = {e: {} for e in CENG + (SP,)}
        self.nd = 0

    def new_epoch(self):
        self.ep += 1
        for e in CENG:
            k = (e, self.ep)
            self.sem[k] = self.es.enter_context(self.nc.semaphore("s_%s_%d" % k))
            self.n[k] = 0
            self.sig[k] = [False]

    def dma_sem(self, name):
        self.nd += 1
        return [self.es.enter_context(self.nc.semaphore(name)), 0, self.nd]

    def _deps(self, eng, reads, writes):
        waits = {}
        seen = self.seen[eng]

        def need(ev):
            if ev is None:
                return
            if ev[0] == "eng":
                e2, val = ev[1], ev[2]
                if e2[0] == eng and eng == PE:
                    return
                key = e2
            else:
                ds, val = ev[1], ev[2]
                key = ("d", ds[2])
            if seen.get(key, 0) >= val:
                return
            if key not in waits or waits[key][2] < val:
                waits[key] = ev

        for b in reads:
            need(b.w)
        for b in writes:
            need(b.w)
            for r in b.rs.values():
                need(r)
        for key, ev in waits.items():
            seen[key] = ev[2]
            if ev[0] == "eng":
                self.sig[ev[1]][ev[2]] = True
        return list(waits.values())

    def op(self, eng, fn, reads=(), writes=()):
        waits = self._deps(eng, reads, writes)
        k = (eng, self.ep)
        self.n[k] += 1
        self.sig[k].append(os.environ.get("LAZY", "1") != "1")
        ev = ("eng", k, self.n[k])
        self.q[eng].append((waits, fn, ev))
        for b in reads:
            b.rs[eng] = ev
        for b in writes:
            b.w = ev
            b.rs = {}

    def dma(self, qeng, fn, dsem, reads=(), writes=()):
        waits = self._deps(qeng, reads, writes)
        dsem[1] += 16
        ev = ("dma", dsem, dsem[1])
        self.q[qeng].append((waits, fn, ev))
        for b in reads:
            b.rs[("d", dsem[2])] = ev
        for b in writes:
            b.w = ev
            b.rs = {}

    def fence(self):
        last = []
        for e in CENG:
            k = (e, self.ep)
            if self.n[k] > 0:
                b = Buf()
                b.w = ("eng", k, self.n[k])
                last.append(b)
        for e in CENG + (SP,):
            self.wait_all(e, last)

    def wait_all(self, eng, blist):
        waits = self._deps(eng, blist, ())
        self.q[eng].append((waits, None, None))

    def emit(self):
        cum = {}
        for e in self.n:
            c = [0] * (self.n[e] + 1)
            for i in range(1, self.n[e] + 1):
                c[i] = c[i - 1] + (1 if self.sig[e][i] else 0)
            cum[e] = c
        self.maxcount = {e: cum[e][-1] for e in cum}
        with self.nc.Block() as block:
            def mkb(e):
                def body(engobj):
                    for waits, fn, ev in self.q[e]:
                        for w in waits:
                            if w[0] == "eng":
                                engobj.wait_ge(self.sem[w[1]], cum[w[1]][w[2]])
                            else:
                                engobj.wait_ge(w[1][0], w[2])
                        if fn is not None:
                            ins = fn(engobj)
                            if ev[0] == "dma":
                                ins.then_inc(ev[1][0], 16)
                            elif self.sig[ev[1]][ev[2]]:
                                ins.then_inc(self.sem[ev[1]], 1)
                return body

            block.tensor(mkb(PE))
            block.scalar(mkb(ACT))
            block.vector(mkb(DVE))
            block.gpsimd(mkb(POOL))
            block.sync(mkb(SP))


def mk(f, *a, **k):
    return lambda e: getattr(e, f)(*a, **k)


IN_OFF = dict(q=0, k=384, v=768, z=1152, xbc=1536, dt=2432, p=2438)


def in_tile_cols():
    tiles = []
    for j in range(3):
        cols = (list(range(IN_OFF["k"] + 128 * j, IN_OFF["k"] + 128 * (j + 1)))
                + list(range(IN_OFF["q"] + 128 * j, IN_OFF["q"] + 128 * (j + 1)))
                + list(range(IN_OFF["v"] + 128 * j, IN_OFF["v"] + 128 * (j + 1))))
        tiles.append(("A%d" % j, cols))
    x0 = IN_OFF["xbc"]
    tiles.append(("X0", list(range(x0, x0 + 384))))
    tiles.append(("X1", list(range(x0 + 384, x0 + 768))))
    tiles.append(("X2", list(range(x0 + 768, x0 + 896))))
    tiles.append(("ZD", list(range(IN_OFF["z"], IN_OFF["z"] + 384)) + list(range(IN_OFF["dt"], IN_OFF["dt"] + 6))))
    tiles.append(("PP", list(range(IN_OFF["p"], IN_OFF["p"] + 256))))
    return tiles


OUT_TILES = [(0, 384), (384, 768), (768, 1024)]


def layer_tile_plan():
    plan = []
    for name, cols in in_tile_cols():
        plan.append((name, 8, 400 if name == "ZD" else len(cols)))
    for i, (a, b) in enumerate(OUT_TILES):
        plan.append(("O%d" % i, 8, b - a))
    for pi, part in enumerate(FFN_PARTS):
        for j in part:
            plan.append(("G%d" % j, 8, 256))
        for cp in range(4):
            plan.append(("D%d_%d" % (pi, cp), len(part), 256))
    return plan


def pack_weights(w_in, w_out, w_gate_up, w_down, L):
    plan = layer_tile_plan()
    chunks = []
    for l in range(L):
        intiles = dict(in_tile_cols())
        for name, K, n in plan:
            if name in intiles:
                W = np.zeros((D, n), np.float32)
                W[:, :len(intiles[name])] = w_in[l][:, intiles[name]]
                t = W.reshape(8, 128, n).transpose(1, 0, 2)
            elif name[0] == "O":
                a, b = OUT_TILES[int(name[1:])]
                t = w_out[l][:, a:b].reshape(8, 128, n).transpose(1, 0, 2)
            elif name[0] == "G":
                j = int(name[1:])
                W = np.concatenate([w_gate_up[l][:, j * 128:(j + 1) * 128],
                                    w_gate_up[l][:, FFN + j * 128:FFN + (j + 1) * 128]], axis=1)
                t = W.reshape(8, 128, 256).transpose(1, 0, 2)
            else:
                pi, cp = [int(v) for v in name[1:].split("_")]
                part = FFN_PARTS[pi]
                W = w_down[l][part[0] * 128:(part[-1] + 1) * 128, cp * 256:(cp + 1) * 256]
                t = W.reshape(len(part), 128, 256).transpose(1, 0, 2)
            chunks.append(np.ascontiguousarray(t).reshape(128, K * n))
    return np.ascontiguousarray(np.concatenate(chunks, axis=1), dtype=np.float32)


SMF_PER = 8 + 8 + 28 + 7 + 2
ROW_PER = 6 + 6 + 384 + 384


def pack_small(norm_mix, norm_ffn, conv_w, conv_b, pool_scale, norm_final, dt_bias, a_log, d_skip, ssd_norm, pool_w, L):
    smf = np.zeros((128, SMF_PER * L + 8), np.float32)
    rowp = np.zeros((128, ROW_PER * L), np.float32)
    pw = np.zeros((L, 128, 2, 128), np.float32)
    for l in range(L):
        o = SMF_PER * l
        smf[:, o:o + 8] = norm_mix[l].reshape(8, 128).T
        smf[:, o + 8:o + 16] = norm_ffn[l].reshape(8, 128).T
        for j in range(4):
            smf[:, o + 16 + 7 * j:o + 16 + 7 * (j + 1)] = conv_w[l][j].reshape(7, 128).T
        smf[:, o + 44:o + 51] = conv_b[l].reshape(7, 128).T
        smf[:, o + 51:o + 53] = pool_scale[l].reshape(2, 128).T
        r = ROW_PER * l
        rowp[:, r:r + 6] = dt_bias[l][None, :]
        rowp[:, r + 6:r + 12] = a_log[l][None, :]
        rowp[:, r + 12:r + 396] = np.repeat(d_skip[l], 64)[None, :]
        rowp[:, r + 396:r + 780] = ssd_norm[l][None, :]
        for c in range(2):
            for gg in range(2):
                pw[l, gg * 64:(gg + 1) * 64, c, gg * 64:(gg + 1) * 64] = pool_w[l][2 * c + gg]
    smf[:, SMF_PER * L:SMF_PER * L + 8] = norm_final.reshape(8, 128).T
    return smf, rowp, pw


class StopBuild(Exception):
    pass


def build_program(L=4, dbg=(), stop=None):
    nc = bass.Bass("TRN2", target_bir_lowering=False)
    plan = layer_tile_plan()
    lay_words = sum(K * n for _, K, n in plan)
    xT_d = nc.dram_tensor("xT", [D, S], F32, kind="ExternalInput").ap()
    wst_d = nc.dram_tensor("wst", [128, lay_words * L], F32, kind="ExternalInput").ap()
    smf_d = nc.dram_tensor("smf", [128, SMF_PER * L + 8], F32, kind="ExternalInput").ap()
    rowp_d = nc.dram_tensor("rowp", [128, ROW_PER * L], F32, kind="ExternalInput").ap()
    pw_d = nc.dram_tensor("pw", [L, 128, 2, 128], F32, kind="ExternalInput").ap()
    outT_d = nc.dram_tensor("outT", [D, S], F32, kind="ExternalOutput").ap()
    dbg_d = {}
    for name, shape in dbg:
        dbg_d[name] = nc.dram_tensor("dbg_" + name, list(shape), F32, kind="ExternalOutput").ap()

    with ExitStack() as es:
        P = Prog(nc, es)

        def sb(name, shape, dt):
            return es.enter_context(nc.sbuf_tensor(name, shape, dt))

        xT = sb("xT_sb", [128, 8, S], F32)
        xT_b = [[Buf() for _ in range(NG)] for _ in range(8)]
        hT = sb("hT_sb", [128, 8, S], BF16)
        hT_b = [[Buf() for _ in range(NG)] for _ in range(8)]
        NSLOT = 3
        SLOTW = 3200
        wsl = [sb("wslot%d" % i, [128, SLOTW], BF16) for i in range(NSLOT)]
        wsl_b = [Buf() for _ in range(NSLOT)]
        wsem = [P.dma_sem("wsem%d" % i) for i in range(NSLOT)]
        R2 = sb("R2", [128, 10, S], BF16)
        R2_b = [[Buf() for _ in range(NT)] for _ in range(10)]
        R1W = 9728
        R1 = sb("R1", [128, R1W], F32)
        GR = 64
        R1_b = [Buf() for _ in range(R1W // GR)]
        smf = sb("smf_sb", [128, SMF_PER * L + 8], F32)
        rowp = sb("rowp_sb", [128, ROW_PER], F32)
        cR = Buf("rowp")
        rsem = P.dma_sem("rsem")
        pwb = sb("pw_sb", [128, L, 2, 128], BF16)
        ident = sb("ident", [128, 128], BF16)
        onesb = sb("onesb", [128, 128], BF16)
        onesf = sb("onesf", [128, 128], F32)
        triLE = sb("triLE", [128, 128], F32)
        triLEb = sb("triLEb", [128, 128], BF16)
        triGT = sb("triGT", [128, 128], BF16)
        indB = sb("indB", [64, 8, 128], BF16)
        invcnt = sb("invcnt", [128, 2, 16], F32)
        invw = sb("invw", [128, 2], F32)
        zcol = sb("zcol", [128, 2], F32)
        cB = Buf("consts")
        cS = Buf("consts_sp")
        cP = Buf("consts_pool_dma")
        cA = [cB, cS, cP]
        ld = P.dma_sem("ld")
        ldp = P.dma_sem("ldp")
        st = P.dma_sem("st")
        sts = [P.dma_sem("st%d" % i) for i in range(3)]
        banks = [es.enter_context(nc.psum_tensor("bank%d" % i, [128, 512], F32)) for i in range(8)]
        bank_b = [Buf() for _ in range(8)]

        def r1(offw, nwords, dt=F32):
            ap = R1[:, offw:offw + nwords]
            if dt != F32:
                ap = ap.bitcast(dt)
            return ap, R1_b[offw // GR:(offw + nwords + GR - 1) // GR]

        wq = []
        off = 0
        for l in range(L):
            for name, K, n in plan:
                wq.append((l, name, K, n, off))
                off += K * n
        wstate = {"next": 0, "cur": {}}

        def w_issue():
            i = wstate["next"]
            if i >= len(wq):
                return
            l, name, K, n, off = wq[i]
            s = i % NSLOT
            P.dma(POOL, mk("dma_start", out=wsl[s][:, 0:K * n], in_=wst_d[:, off:off + K * n]), wsem[s], writes=[wsl_b[s]])
            wstate["cur"][(l, name)] = (s, K, n)
            wstate["next"] = i + 1

        def w_get(l, name):
            s, K, n = wstate["cur"][(l, name)]
            return wsl[s][:, 0:K * n].rearrange("p (k n) -> p k n", k=K), wsl_b[s]

        def w_done():
            w_issue()

        for c in range(8):
            P.dma(SP, mk("dma_start", out=xT[:, c, :], in_=xT_d[c * 128:(c + 1) * 128, :]), ld, writes=xT_b[c])
        P.dma(SP, mk("dma_start", out=smf[:], in_=smf_d), ld, writes=[cS])
        for l in range(L):
            P.dma(POOL, mk("dma_start", out=pwb[:, l, :, :], in_=pw_d[l]), ldp, writes=[cP])
        for _ in range(NSLOT):
            w_issue()
        for bb_ in [cS] + [b for c in range(8) for b in xT_b[c]]:
            bb_.w = ("dma", ld, ld[1])
        cP.w = ("dma", ldp, ldp[1])
        P.op(POOL, mk("memset", onesf[:], 1.0), writes=[cB])
        P.op(POOL, mk("memset", zcol[:], 0.0), writes=[cB])
        P.op(POOL, mk("memset", onesb[:], 1.0), writes=[cB])
        P.op(POOL, mk("affine_select", out=ident[:], in_=onesf[:], pattern=[[-1, 128]], compare_op=ALU.is_equal, fill=0.0, base=0, channel_multiplier=1), reads=cA, writes=[cB])
        P.op(POOL, mk("affine_select", out=triLE[:], in_=onesf[:], pattern=[[1, 128]], compare_op=ALU.is_ge, fill=0.0, base=0, channel_multiplier=-1), reads=cA, writes=[cB])
        P.op(POOL, mk("affine_select", out=triLEb[:], in_=onesf[:], pattern=[[1, 128]], compare_op=ALU.is_ge, fill=0.0, base=0, channel_multiplier=-1), reads=cA, writes=[cB])
        P.op(POOL, mk("affine_select", out=triGT[:], in_=onesf[:], pattern=[[-1, 128]], compare_op=ALU.is_gt, fill=0.0, base=0, channel_multiplier=1), reads=cA, writes=[cB])
        P.op(POOL, mk("memset", indB[:], 1.0), writes=[cB])
        for a in range(2):
            P.op(POOL, mk("affine_select", out=indB[32 * a:32 * a + 32, :, :], in_=indB[32 * a:32 * a + 32, :, :],
                             pattern=[[-1, 8], [0, 128]], compare_op=ALU.is_equal, fill=0.0, base=0, channel_multiplier=1), reads=cA, writes=[cB])
        for c in range(2):
            for hh in range(2):
                wv = float(2 ** (2 * c + hh + 1))
                P.op(POOL, mk("memset", invw[64 * hh:64 * hh + 64, c:c + 1], 1.0 / wv), writes=[cB])
                P.op(POOL, mk("iota", invcnt[64 * hh:64 * hh + 64, c, :], [[1, 16]], base=1, channel_multiplier=0, allow_small_or_imprecise_dtypes=True), writes=[cB])
                P.op(POOL, mk("tensor_scalar_min", invcnt[64 * hh:64 * hh + 64, c, :], invcnt[64 * hh:64 * hh + 64, c, :], wv), reads=cA, writes=[cB])
        P.op(DVE, mk("reciprocal", invcnt[:], invcnt[:]), reads=cA, writes=[cB])

        biasT, biasT_b = r1(3088, 1024, BF16)
        stage, stage_b = r1(6944, 32, BF16)

        def build_stage_const(blk):
            sv = stage.rearrange("p (a n) -> p a n", a=2)
            P.op(DVE, mk("memset", stage, 0.0), writes=stage_b)
            if blk < 7:
                P.op(DVE, mk("memset", sv[:, :, blk + 1:8], -BIG), writes=stage_b)

        def stage_to_biasT(t):
            bk, bb = banks[7][:].bitcast(BF16), bank_b[7]
            P.op(PE, mk("transpose", bk[0:64, 0:128], stage, ident[:]), reads=stage_b + cA, writes=[bb])
            P.op(DVE, mk("tensor_copy", biasT[0:64, t * 128:(t + 1) * 128], bk[0:64, 0:128]), reads=[bb], writes=biasT_b)


        def rmsnorm_to(gcol0, sink):
            for g in range(NG):
                gs = slice(g * 512, (g + 1) * 512)
                bk, bb = banks[g % 2], bank_b[g % 2]
                for c in range(8):
                    sq, sqb = r1(256 * (c % 2), 256, BF16)
                    P.op(ACT, mk("activation", out=sq, in_=xT[:, c, gs], func=AF.Square), reads=[xT_b[c][g]], writes=sqb)
                    P.op(PE, mk("matmul", bk[:], onesb[:], sq, start=(c == 0), stop=(c == 7)), reads=sqb + cA, writes=[bb])
                lr, lrb = r1(512 + 512 * (g % 2), 512)
                P.op(ACT, mk("activation", out=lr, in_=bk[:], func=AF.Ln, bias=EPS, scale=1.0 / D), reads=[bb], writes=lrb)
                P.op(ACT, mk("activation", out=lr, in_=lr, func=AF.Exp, scale=-0.5), reads=lrb, writes=lrb)
                for c in range(8):
                    sink(c, g, gs, lr, lrb)

        def norm_to_hT(gcol0):
            def sink(c, g, gs, lr, lrb):
                P.op(DVE, mk("scalar_tensor_tensor", out=hT[:, c, gs], in0=xT[:, c, gs], scalar=smf[:, gcol0 + c:gcol0 + c + 1], in1=lr, op0=ALU.mult, op1=ALU.mult),
                     reads=[xT_b[c][g]] + cA + lrb, writes=[hT_b[c][g]])
            rmsnorm_to(gcol0, sink)

        pstate = {"i": 0}

        def proj_fm(wv, wb, col0, g, bankpool):
            i = pstate["i"]
            pstate["i"] += 1
            bi = bankpool[i % len(bankpool)]
            gs = slice(g * 512, (g + 1) * 512)
            for k in range(8):
                P.op(PE, mk("matmul", banks[bi][:], wv[:, k, col0:col0 + 128], hT[:, k, gs], start=(k == 0), stop=(k == 7)),
                     reads=[wb, hT_b[k][g]], writes=[bank_b[bi]])
            return banks[bi], bank_b[bi]

        def proj_tm(wv, wb, col0, ncols, t, bankpool):
            i = pstate["i"]
            pstate["i"] += 1
            bi = bankpool[i % len(bankpool)]
            for k in range(8):
                P.op(PE, mk("matmul", banks[bi][:, 0:ncols], hT[:, k, t * 128:(t + 1) * 128], wv[:, k, col0:col0 + ncols], start=(k == 0), stop=(k == 7)),
                     reads=[wb, hT_b[k][t // 4]], writes=[bank_b[bi]])
            return banks[bi], bank_b[bi]

        dsems = {}

        def dump(name, ap, rb):
            if name in dbg_d and name not in dsems and not os.environ.get("NODUMP"):
                dsems[name] = P.dma_sem("dsem_" + name)
                P.dma(POOL, mk("dma_start", out=dbg_d[name], in_=ap), dsems[name], reads=rb)

        lcur = [0]

        def ckpt(name):
            if stop == name or stop == "%s@%d" % (name, lcur[0]):
                raise StopBuild()

        try:
          for l in range(L):
              lcur[0] = l
              import os
              if l > 0 and os.environ.get("EPOCH", "1") == "1":
                  P.new_epoch()
              so = SMF_PER * l
              P.dma(SP, mk("dma_start", out=rowp[:], in_=rowp_d[:, ROW_PER * l:ROW_PER * (l + 1)]), rsem, writes=[cR])
              ro = 0
              norm_to_hT(so)

              ckpt("A")
              QT, QT_b = r1(0, 1024, BF16)
              KT, KT_b = r1(1024, 1024, BF16)
              VA, VA_b = r1(2048, 1040, BF16)
              VAv = VA.rearrange("p (t h e) -> p t h e", t=NT, h=2)
              ytok, ytok_b = r1(5136, 1024, BF16)
              ytokv = ytok.rearrange("p (t e) -> p t e", t=NT)
              kmT, kmT_b = r1(6928, 4, BF16)
              gate, gate_b = r1(6976, 16)
              kmf, kmf_b = r1(6992, 8)
              cmp_, cmp_b = r1(7000, 128)
              rank, rank_b = r1(7128, 16)
              rden, rden_b = r1(7144, 4)
              P.op(DVE, mk("memset", VAv[:, :, :, 64:65], 1.0), writes=VA_b)
              for t in range(8):
                  if t % 2 == 0:
                      build_stage_const(t // 2)
                  stage_to_biasT(t)
              for j in range(3):
                  wv, wb = w_get(l, "A%d" % j)
                  for g in range(NG):
                      gs = slice(g * 512, (g + 1) * 512)
                      bk, bb = proj_fm(wv, wb, 0, g, [0, 1, 2])
                      P.op(ACT, mk("activation", out=KT[:, gs], in_=bk[:], func=AF.Copy), reads=[bb], writes=KT_b[4 * g:4 * g + 4])
                  for g in range(NG):
                      gs = slice(g * 512, (g + 1) * 512)
                      bk, bb = proj_fm(wv, wb, 128, g, [0, 1, 2])
                      P.op(ACT, mk("activation", out=QT[:, gs], in_=bk[:], func=AF.Copy), reads=[bb], writes=QT_b[4 * g:4 * g + 4])
                  vstate = [0]

                  def vproj(n):
                      for _ in range(n):
                          t_ = vstate[0]
                          if t_ >= NT:
                              return
                          bk, bb = proj_tm(wv, wb, 256, 128, t_, [0, 1, 2])
                          P.op(DVE, mk("tensor_copy", VAv[:, t_, :, 0:64], bk[:, 0:128].rearrange("p (h e) -> p h e", h=2)), reads=[bb], writes=VA_b)
                          vstate[0] += 1
                  P.op(DVE, mk("tensor_reduce", out=kmf, in_=KT.rearrange("p (n s) -> p n s", n=8), axis=AX.X, op=ALU.add), reads=KT_b, writes=kmf_b)
                  P.op(DVE, mk("tensor_copy", kmT, kmf), reads=kmf_b, writes=kmT_b)
                  for t in range(8, NT):
                      blk = t // 2
                      gk, gb = banks[7], bank_b[7]
                      for a in range(2):
                          P.op(PE, mk("matmul", gk[:, 8 * a:8 * a + blk], QT[64 * a:64 * a + 64, t * 128:(t + 1) * 128], kmT[64 * a:64 * a + 64, 0:blk], start=True, stop=True),
                               reads=QT_b + kmT_b, writes=[gb])
                      gv = gate.rearrange("p (a n) -> p a n", a=2)
                      P.op(DVE, mk("tensor_copy", gv[:, :, 0:blk], gk[:, 0:16].rearrange("p (a n) -> p a n", a=2)[:, :, 0:blk]), reads=[gb], writes=gate_b)
                      cv = cmp_.rearrange("p (a n m) -> p a n m", a=2, n=8)[:, :, 0:blk, 0:blk]
                      P.op(DVE, mk("tensor_tensor", out=cv, in0=gv[:, :, 0:blk].unsqueeze(2).to_broadcast([128, 2, blk, blk]),
                                   in1=gv[:, :, 0:blk].unsqueeze(3).to_broadcast([128, 2, blk, blk]), op=ALU.is_gt), reads=gate_b, writes=cmp_b)
                      rv = rank.rearrange("p (a n) -> p a n", a=2)
                      P.op(DVE, mk("tensor_reduce", out=rv[:, :, 0:blk], in_=cv, axis=AX.X, op=ALU.add), reads=cmp_b, writes=rank_b)
                      P.op(DVE, mk("tensor_scalar", out=rv[:, :, 0:blk], in0=rv[:, :, 0:blk], scalar1=2.5, scalar2=BIG, op0=ALU.is_lt, op1=ALU.mult), reads=rank_b, writes=rank_b)
                      sv = stage.rearrange("p (a n) -> p a n", a=2)
                      if t % 2 == 0:
                          build_stage_const(blk)
                      P.op(DVE, mk("tensor_scalar_add", sv[:, :, 0:blk], rv[:, :, 0:blk], -BIG), reads=rank_b, writes=stage_b)
                      vproj(2)
                      stage_to_biasT(t)
                  vproj(NT)
                  w_done()
                  if l == 0 and j == 0:
                      dump("KT0", KT, KT_b)
                      dump("QT0", QT, QT_b)
                  items = []
                  for a in range(2):
                      for g in range(NG):
                          for kt in range(4 * (g + 1)):
                              items.append((a, g, kt))
                  import os
                  LOOK = int(os.environ.get("ATT_LOOK", "1"))
                  NSB = LOOK + 1

                  def stage_s(i):
                      a, g, kt = items[i]
                      pr = slice(64 * a, 64 * a + 64)
                      br = slice(32 * a, 32 * a + 8)
                      qlo = max(kt, 4 * g)
                      nq = 4 * g + 4 - qlo
                      qs = slice(qlo * 128, (4 * g + 4) * 128)
                      sk, sbb = banks[5 + (i % NSB)], bank_b[5 + (i % NSB)]
                      P.op(PE, mk("matmul", sk[:, 0:nq * 128], KT[pr, kt * 128:(kt + 1) * 128], QT[pr, qs], start=True, stop=False),
                           reads=KT_b + QT_b, writes=[sbb])
                      P.op(PE, mk("matmul", sk[:, 0:nq * 128], indB[br, kt // 2, :], biasT[br, qs], start=False, stop=True),
                           reads=cA + biasT_b, writes=[sbb])
                      pT, pT_b = r1(6160 + 256 * (i % 3), 256, BF16)
                      P.op(ACT, mk("activation", out=pT[:, 0:nq * 128], in_=sk[:, 0:nq * 128], func=AF.Exp, scale=0.125), reads=[sbb], writes=pT_b)
                      if kt >= 4 * g:
                          P.op(DVE, mk("tensor_tensor", out=pT[:, 0:128], in0=pT[:, 0:128], in1=triLEb[:], op=ALU.mult), reads=pT_b + cA, writes=pT_b)

                  def stage_v(i):
                      a, g, kt = items[i]
                      rnd = a * NG + g
                      nkt = 4 * (g + 1)
                      qlo = max(kt, 4 * g)
                      nq = 4 * g + 4 - qlo
                      ok_, ob = banks[3 + (rnd % 2)], bank_b[3 + (rnd % 2)]
                      okv = ok_[:, 0:260].rearrange("p (q e) -> p q e", q=4)
                      pT, pT_b = r1(6160 + 256 * (i % 3), 256, BF16)
                      for qi in range(nq):
                          qt = qlo + qi
                          P.op(PE, mk("matmul", okv[:, qt - 4 * g, :], pT[:, qi * 128:(qi + 1) * 128], VAv[:, kt, a, :], start=(kt == 0 and qi == 0), stop=(kt == nkt - 1 and qi == nq - 1), skip_group_check=True),
                               reads=pT_b + VA_b, writes=[ob])
                      if kt == nkt - 1:
                          rd, rd_b = r1(7144 + 4 * (rnd % 2), 4)
                          rv4 = rd.rearrange("p (q o) -> p q o", q=4)
                          P.op(DVE, mk("reciprocal", rv4, okv[:, :, 64:65]), reads=[ob], writes=rd_b)
                          P.op(DVE, mk("tensor_tensor", out=ytokv[:, 4 * g:4 * g + 4, 64 * a:64 * a + 64], in0=okv[:, :, 0:64], in1=rv4.to_broadcast([128, 4, 64]), op=ALU.mult),
                               reads=[ob] + rd_b, writes=ytok_b)

                  for i in range(min(LOOK, len(items))):
                      stage_s(i)
                  for i in range(len(items)):
                      if i + LOOK < len(items):
                          stage_s(i + LOOK)
                      stage_v(i)
                  for t in range(NT):
                      bk, bb = banks[7][:].bitcast(BF16), bank_b[7]
                      P.op(PE, mk("transpose", bk[:, 0:128], ytokv[:, t, :], ident[:]), reads=ytok_b + cA, writes=[bb])
                      P.op(ACT, mk("activation", out=R2[:, j, t * 128:(t + 1) * 128], in_=bk[:, 0:128], func=AF.Copy), reads=[bb], writes=[R2_b[j][t]])

              ckpt("attn")
              sz, sz_b = r1(0, 3072, BF16)
              szv = sz.rearrange("p (t e) -> p t e", t=NT)
              dtraw, dtraw_b = r1(3072, 96)
              dtv = dtraw.rearrange("p (t h) -> p t h", t=NT)
              cbase = 3200
              xci = 0
              for name, nch in (("X0", 3), ("X1", 3), ("X2", 1)):
                  wv, wb = w_get(l, name)
                  for cc in range(nch):
                      ci = xci + cc
                      carry = None
                      for g in range(NG):
                          gs = slice(g * 512, (g + 1) * 512)
                          bk, bb = proj_fm(wv, wb, cc * 128, g, [0, 1, 2])
                          xp, xp_b = r1(cbase + 520 * (g % 2), 520)
                          if g == 0:
                              P.op(DVE, mk("memset", xp[:, 0:3], 0.0), writes=xp_b)
                          else:
                              P.op(DVE, mk("tensor_copy", xp[:, 0:3], carry[0][:, 512:515]), reads=carry[1], writes=xp_b)
                          P.op(ACT, mk("activation", out=xp[:, 3:515], in_=bk[:], func=AF.Copy), reads=[bb], writes=xp_b)
                          acc, acc_b = r1(cbase + 1040 + 512 * (g % 2), 512)
                          cw = so + 16
                          ceng = DVE
                          P.op(ACT, mk("activation", out=acc, in_=bk[:], func=AF.Copy, scale=smf[:, cw + 21 + ci:cw + 22 + ci]), reads=[bb] + cA, writes=acc_b)
                          for tap in range(0, 3):
                              P.op(ceng, mk("scalar_tensor_tensor", out=acc, in0=xp[:, tap:tap + 512], scalar=smf[:, cw + 7 * tap + ci:cw + 7 * tap + ci + 1], in1=acc, op0=ALU.mult, op1=ALU.add),
                                   reads=xp_b + acc_b + cA, writes=acc_b)
                          P.op(ACT, mk("activation", out=R2[:, 3 + ci, gs], in_=acc, func=AF.Silu, bias=smf[:, so + 44 + ci:so + 45 + ci]), reads=acc_b + cA, writes=R2_b[3 + ci][4 * g:4 * g + 4])
                          carry = (xp, xp_b)
                  xci += nch
                  w_done()
              ckpt("b2x")
              wv, wb = w_get(l, "ZD")
              for t in range(NT):
                  bk, bb = proj_tm(wv, wb, 0, 390, t, [0, 1, 2])
                  P.op(ACT, mk("activation", out=szv[:, t, :], in_=bk[:, 0:384], func=AF.Silu, bias=zcol[:, 0:1]), reads=[bb] + cA, writes=sz_b)
                  P.op(DVE, mk("tensor_copy", dtv[:, t, :], bk[:, 384:390]), reads=[bb], writes=dtraw_b)
              w_done()

              ckpt("b2")
              o = cbase
              dt_, dt_b = r1(o, 96); o += 128
              dA, dA_b = r1(o, 96); o += 128
              arow, arow_b = r1(o, 8); o += 64
              Rh, Rh_b = r1(o, 384, BF16); o += 384
              Rl, Rl_b = r1(o, 384, BF16); o += 384
              E1s = [r1(o + 384 * i, 384, BF16) for i in range(2)]; o += 768
              Gm, Gm_b = r1(o, 256); o += 256
              MTs = [r1(o + 384 * i, 384, BF16) for i in range(2)]; o += 768
              xdts = [r1(o + 192 * i, 192, BF16) for i in range(2)]; o += 384
              xdds = [r1(o + 192 * i, 192, BF16) for i in range(2)]; o += 384
              xDs = [r1(o, 384)] * 2; o += 384
              Btoks = [r1(o + 128 * i, 128, BF16) for i in range(2)]; o += 256
              yc, yc_b = r1(o, 384); o += 384
              y3s, y3s_b = r1(o, 192, BF16); o += 192
              junk, junk_b = r1(o, 192, BF16); o += 192
              prev, prev_b = r1(o, 384); o += 384
              pbfs = [r1(o + 192 * i, 192, BF16) for i in range(2)]; o += 384
              w2s = [r1(o + 64 * i, 8) for i in range(2)]; o += 128
              eacds = [r1(o + 64 * i, 12) for i in range(2)]; o += 128
              ss, ss_b = r1(o, 4); o += 64
              w2off = o; o += 256
              assert o <= R1W, o
              dtall = dt_.rearrange("p (t h) -> p t h", t=NT)
              dAall = dA.rearrange("p (t h) -> p t h", t=NT)
              P.op(DVE, mk("tensor_tensor", out=dtall, in0=dtv, in1=rowp[:, ro:ro + 6].unsqueeze(1).to_broadcast([128, NT, 6]), op=ALU.add), reads=dtraw_b + cA + [cR], writes=dt_b)
              P.op(ACT, mk("activation", out=dt_, in_=dt_, func=AF.Exp), reads=dt_b, writes=dt_b)
              P.op(ACT, mk("activation", out=dt_, in_=dt_, func=AF.Ln, bias=1.0), reads=dt_b, writes=dt_b)
              P.op(ACT, mk("activation", out=arow[:, 0:6], in_=rowp[:, ro + 6:ro + 12], func=AF.Exp), reads=cA + [cR], writes=arow_b)
              P.op(DVE, mk("scalar_tensor_tensor", out=dAall, in0=dtall, scalar=-1.0, in1=arow[:, 0:6].unsqueeze(1).to_broadcast([128, NT, 6]), op0=ALU.mult, op1=ALU.mult), reads=dt_b + arow_b, writes=dA_b)
              dAh, dAh_b = r1(w2off + 128, 48, BF16)
              dAl, dAl_b = r1(w2off + 192, 48, BF16)
              P.op(DVE, mk("tensor_copy", dAh, dA), reads=dA_b, writes=dAh_b)
              P.op(DVE, mk("tensor_tensor", out=dAl, in0=dA, in1=dAh, op=ALU.subtract), reads=dA_b + dAh_b, writes=dAl_b)
              dAhv = dAh.rearrange("p (t h) -> p t h", t=NT)
              dAlv = dAl.rearrange("p (t h) -> p t h", t=NT)
              P.op(DVE, mk("memset", prev, 0.0), writes=prev_b)
              def ssd_front(t):
                  ts_ = slice(t * 128, (t + 1) * 128)
                  E1, E1_b = E1s[t % 2]
                  MT, MT_b = MTs[t % 2]
                  xdt, xdt_b = xdts[t % 2]
                  xdd, xdd_b = xdds[t % 2]
                  xD, xD_b = xDs[t % 2]
                  Btok, Btok_b = Btoks[t % 2]
                  eacd, eacd_b = eacds[t % 2]
                  w2, w2_b = w2s[t % 2]
                  E1v = E1.rearrange("p (h l) -> p h l", h=6)
                  MTv = MT.rearrange("p (h l) -> p h l", h=6)
                  trk, trb = banks[3][:].bitcast(BF16), bank_b[3]
                  for i in range(5):
                      P.op(PE, mk("transpose", trk[:, i * 128:(i + 1) * 128], R2[:, 3 + i, ts_], ident[:]), reads=[R2_b[3 + i][t]] + cA, writes=[trb])
                  xv = trk[:, 0:384].rearrange("p (h e) -> p h e", h=6)
                  gk, gb = banks[4], bank_b[4]
                  for gg in range(2):
                      P.op(PE, mk("matmul", gk[:, gg * 128:(gg + 1) * 128], R2[:, 6 + gg, ts_], R2[:, 8 + gg, ts_], start=True, stop=True), reads=[R2_b[6 + gg][t], R2_b[8 + gg][t]], writes=[gb])
                  for Rx, Rx_b, dv, dvb in ((Rh, Rh_b, dAhv, dAh_b), (Rl, Rl_b, dAlv, dAl_b)):
                      P.op(DVE, mk("tensor_tensor", out=Rx.rearrange("p (h l) -> p h l", h=6), in0=triLEb[:].unsqueeze(1).to_broadcast([128, 6, 128]), in1=dv[:, t, :].unsqueeze(2).to_broadcast([128, 6, 128]), op=ALU.mult),
                           reads=cA + dvb, writes=Rx_b)
                  d1a, d1b_ = banks[0], banks[1]
                  sm, smb = banks[2][:, 0:12], bank_b[2]
                  for ri, (Rx, Rx_b, dv, dvb) in enumerate(((Rh, Rh_b, dAhv, dAh_b), (Rl, Rl_b, dAlv, dAl_b))):
                      P.op(PE, mk("matmul", d1a[:, 0:384], triGT[:], Rx[:, 0:384], start=(ri == 0), stop=(ri == 1)), reads=cA + Rx_b, writes=[bank_b[0]])
                      P.op(PE, mk("matmul", d1b_[:, 0:384], triGT[:], Rx[:, 384:768], start=(ri == 0), stop=(ri == 1)), reads=cA + Rx_b, writes=[bank_b[1]])
                  for ri, (Rx, Rx_b, dv, dvb) in enumerate(((Rh, Rh_b, dAhv, dAh_b), (Rl, Rl_b, dAlv, dAl_b))):
                      P.op(PE, mk("matmul", sm[:, 0:6], triLEb[:], dv[:, t, :], start=(ri == 0), stop=(ri == 1), skip_group_check=True), reads=cA + dvb, writes=[smb])
                  for ri, (Rx, Rx_b, dv, dvb) in enumerate(((Rh, Rh_b, dAhv, dAh_b), (Rl, Rl_b, dAlv, dAl_b))):
                      P.op(PE, mk("matmul", sm[:, 6:12], onesb[:], dv[:, t, :], start=False, stop=(ri == 1), skip_group_check=True), reads=cA + dvb, writes=[smb])
                  P.op(ACT, mk("activation", out=E1[:, 0:384], in_=d1a[:, 0:384], func=AF.Exp), reads=[bank_b[0]], writes=E1_b)
                  P.op(ACT, mk("activation", out=E1[:, 384:768], in_=d1b_[:, 0:384], func=AF.Exp), reads=[bank_b[1]], writes=E1_b)
                  P.op(ACT, mk("activation", out=eacd, in_=sm[:, 0:12], func=AF.Exp), reads=[smb], writes=eacd_b)
                  P.op(DVE, mk("tensor_tensor", out=xdt.rearrange("p (h e) -> p h e", h=6), in0=xv, in1=dtall[:, t, :].unsqueeze(2).to_broadcast([128, 6, 64]), op=ALU.mult), reads=[trb] + dt_b, writes=xdt_b)
                  P.op(DVE, mk("tensor_tensor", out=xD, in0=trk[:, 0:384], in1=rowp[:, ro + 12:ro + 396], op=ALU.mult), reads=[trb] + cA + [cR], writes=xD_b)
                  P.op(ACT, mk("activation", out=Btok, in_=trk[:, 384:640], func=AF.Copy), reads=[trb], writes=Btok_b)
                  P.op(DVE, mk("tensor_tensor", out=Gm.rearrange("p (g l) -> p g l", g=2), in0=gk[:, 0:256].rearrange("p (g l) -> p g l", g=2), in1=triLE[:].unsqueeze(1).to_broadcast([128, 2, 128]), op=ALU.mult),
                       reads=[gb] + cA, writes=Gm_b)
                  P.op(DVE, mk("tensor_tensor", out=w2[:, 0:6], in0=dtall[:, t, :], in1=E1v[:, :, 127], op=ALU.mult), reads=dt_b + E1_b, writes=w2_b)
                  P.op(DVE, mk("tensor_tensor", out=xdd.rearrange("p (h e) -> p h e", h=6), in0=xv, in1=w2[:, 0:6].unsqueeze(2).to_broadcast([128, 6, 64]), op=ALU.mult), reads=[trb] + w2_b, writes=xdd_b)
                  P.op(DVE, mk("tensor_tensor", out=MT.rearrange("p (g r l) -> p g r l", g=2, r=3), in0=Gm.rearrange("p (g l) -> p g l", g=2).unsqueeze(2).to_broadcast([128, 2, 3, 128]),
                               in1=E1.rearrange("p (g r l) -> p g r l", g=2, r=3), op=ALU.mult), reads=Gm_b + E1_b, writes=MT_b)

              def ssd_back(t):
                  ts_ = slice(t * 128, (t + 1) * 128)
                  E1, E1_b = E1s[t % 2]
                  MT, MT_b = MTs[t % 2]
                  xdt, xdt_b = xdts[t % 2]
                  xdd, xdd_b = xdds[t % 2]
                  xD, xD_b = xDs[t % 2]
                  Btok, Btok_b = Btoks[t % 2]
                  eacd, eacd_b = eacds[t % 2]
                  w2, w2_b = w2s[t % 2]
                  E1v = E1.rearrange("p (h l) -> p h l", h=6)
                  MTv = MT.rearrange("p (h l) -> p h l", h=6)
                  yo, yob = banks[6], bank_b[6]
                  if t > 0:
                      pb, pb_b = pbfs[t % 2]
                      for gg in range(2):
                          P.op(PE, mk("matmul", yo[:, gg * 192:(gg + 1) * 192], R2[:, 8 + gg, ts_], pb[:, gg * 192:(gg + 1) * 192], start=True, stop=True), reads=[R2_b[8 + gg][t]] + pb_b, writes=[yob])
                  if t < NT - 1:
                      sk_, skb = banks[7], bank_b[7]
                      for gg in range(2):
                          P.op(PE, mk("matmul", sk_[:, gg * 192:(gg + 1) * 192], Btok[:, gg * 128:(gg + 1) * 128], xdd[:, gg * 192:(gg + 1) * 192], start=True, stop=True), reads=Btok_b + xdd_b, writes=[skb])
                      pv = prev.rearrange("p (h e) -> p h e", h=6)
                      P.op(DVE, mk("tensor_tensor", out=pv, in0=pv, in1=eacd[:, 6:12].unsqueeze(2).to_broadcast([128, 6, 64]), op=ALU.mult), reads=prev_b + eacd_b, writes=prev_b)
                      P.op(DVE, mk("tensor_tensor", out=prev, in0=prev, in1=sk_[:, 0:384], op=ALU.add), reads=prev_b + [skb], writes=prev_b)
                      nb, nb_b = pbfs[(t + 1) % 2]
                      P.op(ACT, mk("activation", out=nb, in_=prev, func=AF.Copy), reads=prev_b, writes=nb_b)
                  yk, ykb = banks[5], bank_b[5]
                  for h in range(6):
                      P.op(PE, mk("matmul", yk[:, h * 64:(h + 1) * 64], MTv[:, h, :], xdt[:, h * 64:(h + 1) * 64], start=True, stop=True), reads=MT_b + xdt_b, writes=[ykb])
                  ycv = yc.rearrange("p (h e) -> p h e", h=6)
                  if t > 0:
                      P.op(DVE, mk("tensor_tensor", out=ycv, in0=yo[:, 0:384].rearrange("p (h e) -> p h e", h=6), in1=eacd[:, 0:6].unsqueeze(2).to_broadcast([128, 6, 64]), op=ALU.mult), reads=[yob] + eacd_b, writes=yc_b)
                      P.op(DVE, mk("tensor_tensor", out=yc, in0=yc, in1=yk[:, 0:384], op=ALU.add), reads=yc_b + [ykb], writes=yc_b)
                      P.op(DVE, mk("tensor_tensor", out=yc, in0=yc, in1=xD, op=ALU.add), reads=yc_b + xD_b, writes=yc_b)
                  else:
                      P.op(DVE, mk("tensor_tensor", out=yc, in0=xD, in1=yk[:, 0:384], op=ALU.add), reads=xD_b + [ykb], writes=yc_b)
                  P.op(DVE, mk("tensor_tensor", out=yc, in0=yc, in1=szv[:, t, :], op=ALU.mult), reads=yc_b + sz_b, writes=yc_b)
                  for gg in range(2):
                      P.op(ACT, mk("activation", out=junk[:, 0:192], in_=yc[:, gg * 192:(gg + 1) * 192], func=AF.Square, accum_out=ss[:, gg:gg + 1]), reads=yc_b, writes=junk_b + ss_b)
                  P.op(ACT, mk("activation", out=ss[:, 0:2], in_=ss[:, 0:2], func=AF.Ln, bias=EPS, scale=1.0 / 192), reads=ss_b, writes=ss_b)
                  P.op(ACT, mk("activation", out=ss[:, 0:2], in_=ss[:, 0:2], func=AF.Exp, scale=-0.5), reads=ss_b, writes=ss_b)
                  for gg in range(2):
                      P.op(DVE, mk("scalar_tensor_tensor", out=y3s[:, gg * 192:(gg + 1) * 192], in0=yc[:, gg * 192:(gg + 1) * 192], scalar=ss[:, gg:gg + 1], in1=rowp[:, ro + 396 + gg * 192:ro + 396 + (gg + 1) * 192],
                                   op0=ALU.mult, op1=ALU.mult), reads=yc_b + ss_b + cA + [cR], writes=y3s_b)
                  tk2, tb2 = banks[2][:].bitcast(BF16), bank_b[2]
                  for i in range(3):
                      P.op(PE, mk("transpose", tk2[:, i * 128:(i + 1) * 128], y3s[:, i * 128:(i + 1) * 128], ident[:]), reads=y3s_b + cA, writes=[tb2])
                  P.op(ACT, mk("activation", out=R2[:, 3:6, ts_], in_=tk2[:, 0:384].rearrange("p (i e) -> p i e", i=3), func=AF.Copy), reads=[tb2], writes=[R2_b[3][t], R2_b[4][t], R2_b[5][t]])


              if os.environ.get("SSDPIPE", "0") == "1":
                  ssd_front(0)
                  for t in range(NT):
                      if t + 1 < NT:
                          ssd_front(t + 1)
                      ssd_back(t)
              else:
                  for t in range(NT):
                      ssd_front(t)
                      ssd_back(t)

              ckpt("ssd")
              wv, wb = w_get(l, "PP")
              PW = 528
              for c in range(2):
                  nlev = 3 if c == 0 else 5
                  prevl = None
                  for g in range(NG):
                      gs = slice(g * 512, (g + 1) * 512)
                      bk, bb = proj_fm(wv, wb, c * 128, g, [0, 1, 2])
                      lv = [r1(cbase + PW * (2 * i + (g % 2)), PW) for i in range(nlev)]
                      for i, (bf_, bfb) in enumerate(lv):
                          if g == 0:
                              P.op(DVE, mk("memset", bf_[:, 0:16], 0.0), writes=bfb)
                          else:
                              P.op(DVE, mk("tensor_copy", bf_[:, 0:16], prevl[i][0][:, 512:528]), reads=prevl[i][1], writes=bfb)
                      p0, p0_b = lv[0]
                      P.op(ACT, mk("activation", out=p0[:, 16:528], in_=bk[:], func=AF.Copy), reads=[bb], writes=p0_b)
                      for i in range(1, nlev):
                          sh = 2 ** (i - 1)
                          src, srcb = lv[i - 1]
                          dst, dstb = lv[i]
                          P.op(DVE, mk("tensor_tensor", out=dst[:, 16:528], in0=src[:, 16:528], in1=src[:, 16 - sh:528 - sh], op=ALU.add), reads=srcb, writes=dstb)
                      dfb, dfb_b = r1(cbase + PW * 10 + 256 * (g % 2), 256, BF16)
                      tmp, tmp_b = r1(cbase + PW * 10 + 512, 16)
                      for hh in range(2):
                          src, srcb = lv[nlev - 2 + hh]
                          hp = slice(64 * hh, 64 * hh + 64)
                          P.op(DVE, mk("scalar_tensor_tensor", out=dfb[hp, :], in0=src[hp, 16:528], scalar=invw[hp, c:c + 1], in1=p0[hp, 16:528], op0=ALU.mult, op1=ALU.subtract), reads=srcb + p0_b + cA, writes=dfb_b)
                          if g == 0:
                              P.op(DVE, mk("tensor_tensor", out=tmp[hp, :], in0=src[hp, 16:32], in1=invcnt[hp, c, :], op=ALU.mult), reads=srcb + cA, writes=tmp_b)
                              P.op(DVE, mk("tensor_tensor", out=dfb[hp, 0:16], in0=tmp[hp, :], in1=p0[hp, 16:32], op=ALU.subtract), reads=tmp_b + p0_b, writes=dfb_b)
                      mk_, mkb = banks[3 + (g % 2)], bank_b[3 + (g % 2)]
                      P.op(PE, mk("matmul", mk_[:], pwb[:, l, c, :], dfb, start=True, stop=True), reads=cA + dfb_b, writes=[mkb])
                      P.op(ACT, mk("activation", out=R2[:, 6 + c, gs], in_=mk_[:], func=AF.Copy, scale=smf[:, so + 51 + c:so + 52 + c]), reads=[mkb] + cA, writes=R2_b[6 + c][4 * g:4 * g + 4])
                      prevl = lv
              w_done()
              if l == 0:
                  dump("ymixT", R2[:, 0:8, :], [b for s_ in range(8) for b in R2_b[s_]])

              ckpt("pool")
              for i, (a0, a1) in enumerate(OUT_TILES):
                  wv, wb = w_get(l, "O%d" % i)
                  for cc in range((a1 - a0) // 128):
                      c = a0 // 128 + cc
                      for g in range(NG):
                          gs = slice(g * 512, (g + 1) * 512)
                          bi = [0, 1, 2][pstate["i"] % 3]
                          pstate["i"] += 1
                          for k in range(8):
                              P.op(PE, mk("matmul", banks[bi][:], wv[:, k, cc * 128:(cc + 1) * 128], R2[:, k, gs], start=(k == 0), stop=(k == 7)), reads=[wb] + R2_b[k][4 * g:4 * g + 4], writes=[bank_b[bi]])
                          P.op(DVE, mk("tensor_tensor", out=xT[:, c, gs], in0=xT[:, c, gs], in1=banks[bi][:], op=ALU.add), reads=[xT_b[c][g], bank_b[bi]], writes=[xT_b[c][g]])
                  w_done()
              if l == 0:
                  dump("xmid", xT[:], [b for c in range(8) for b in xT_b[c]])

              ckpt("out")
              norm_to_hT(so + 8)

              ckpt("H")
              for pi, part in enumerate(FFN_PARTS):
                  for jl, j in enumerate(part):
                      wv, wb = w_get(l, "G%d" % j)
                      for g in range(NG):
                          gs = slice(g * 512, (g + 1) * 512)
                          gk, gb = proj_fm(wv, wb, 0, g, [0, 1, 2, 3])
                          uk, ub = proj_fm(wv, wb, 128, g, [0, 1, 2, 3])
                          sg, sg_b = r1(256 * (g % 2), 256, BF16)
                          P.op(ACT, mk("activation", out=sg, in_=gk[:], func=AF.Silu, bias=zcol[:, 0:1]), reads=[gb] + cA, writes=sg_b)
                          P.op(DVE, mk("tensor_tensor", out=R2[:, jl, gs], in0=sg, in1=uk[:], op=ALU.mult), reads=sg_b + [ub], writes=R2_b[jl][4 * g:4 * g + 4])
                      w_done()
                  for cp in range(4):
                      wv, wb = w_get(l, "D%d_%d" % (pi, cp))
                      for cc in range(2):
                          c = 2 * cp + cc
                          for g in range(NG):
                              gs = slice(g * 512, (g + 1) * 512)
                              bi = [4, 5, 6, 7][pstate["i"] % 4]
                              pstate["i"] += 1
                              for jl in range(len(part)):
                                  P.op(PE, mk("matmul", banks[bi][:], wv[:, jl, cc * 128:(cc + 1) * 128], R2[:, jl, gs], start=(jl == 0), stop=(jl == len(part) - 1)),
                                       reads=[wb] + R2_b[jl][4 * g:4 * g + 4], writes=[bank_b[bi]])
                              P.op(DVE, mk("tensor_tensor", out=xT[:, c, gs], in0=xT[:, c, gs], in1=banks[bi][:], op=ALU.add), reads=[xT_b[c][g], bank_b[bi]], writes=[xT_b[c][g]])
                      w_done()
              if os.environ.get("FENCE", "0") == "1":
                  P.fence()
              ckpt("ffn")
              if l == 0:
                  dump("xl0", xT[:], [b for c in range(8) for b in xT_b[c]])

        except StopBuild:
            dump("ymixT", R2[:, 0:8, :], [b for s_ in range(8) for b in R2_b[s_]])
            dump("xmid", xT[:], [b for c in range(8) for b in xT_b[c]])

        fo = SMF_PER * L

        def fsink(c, g, gs, lr, lrb):
            ob, ob_b = r1(2048 + 512 * ((c + g) % 3), 512)
            P.op(DVE, mk("scalar_tensor_tensor", out=ob, in0=xT[:, c, gs], scalar=smf[:, fo + c:fo + c + 1], in1=lr, op0=ALU.mult, op1=ALU.mult), reads=[xT_b[c][g]] + cA + lrb, writes=ob_b)
            P.dma(SP, mk("dma_start", out=outT_d[c * 128:(c + 1) * 128, gs], in_=ob), sts[(c + g) % 3], reads=ob_b)

        rmsnorm_to(fo, fsink)
        for ss_ in sts + [st] + list(dsems.values()):
            fin = Buf("fin")
            fin.w = ("dma", ss_, ss_[1])
            P.wait_all(SP, [fin])
        P.emit()
    return nc


def prep_inputs(inputs, L=4):
    f = lambda k: np.asarray(inputs[k], dtype=np.float32)
    wst = pack_weights(f("w_in"), f("w_out"), f("w_gate_up"), f("w_down"), L)
    smf, rowp, pw = pack_small(f("norm_mix"), f("norm_ffn"), f("conv_w"), f("conv_b"), f("pool_scale"), f("norm_final"),
                               f("dt_bias"), f("a_log"), f("d_skip"), f("ssd_norm"), f("pool_w"), L)
    x = f("x")
    maps = []
    for b in range(x.shape[0]):
        maps.append({"xT": np.ascontiguousarray(x[b].T), "wst": wst, "smf": smf, "rowp": rowp, "pw": pw})
    return maps


_CACHE = {}


def kernel(**inputs):
    L = 4
    if L not in _CACHE:
        _CACHE[L] = build_program(L)
    nc = _CACHE[L]
    maps = prep_inputs(inputs, L)
    res = run_bass_kernel_spmd(nc, maps, core_ids=list(range(8)))
    out = np.stack([np.ascontiguousarray(r["outT"].T) for r in res.results], axis=0)
    return out.astype(np.float32)
```
